# Optimizing a Trainium2 kernel written in Bass

```python
import math
import jax, jax.numpy as jnp
from jax import lax
import numpy as np

D_MODEL = 2048
BATCH = 1
SEQ = 8192
DEPTH = 1
DEC_BATCH = 16
DEC_SEQ = 64
PAST_LEN = 4096

CHUNK = 64
SSD_EXPAND = 2
SSD_INNER = SSD_EXPAND * D_MODEL
SSD_HEADDIM = 64
SSD_HEADS = SSD_INNER // SSD_HEADDIM
SSD_GROUPS = 8
SSD_HPG = SSD_HEADS // SSD_GROUPS
SSD_STATE = 128
CONV_W = 4
CONV_DIM = SSD_INNER + 2 * SSD_GROUPS * SSD_STATE
ATT_HEADS = 16
ATT_HEAD_DIM = 128
ATT_WIDTH = ATT_HEADS * ATT_HEAD_DIM
BAND_CHUNKS = 8
BAND_PAST = BAND_CHUNKS * CHUNK
BAND_LEN = BAND_PAST + CHUNK
REL_CLIP = 128
N_REL = 2 * REL_CLIP + 1
N_BRANCH = 2
D_FF = -(-(8 * D_MODEL) // (3 * 256)) * 256
LN_EPS = 1e-5
RMS_EPS = 1e-5
ALPHA = (2.0 * DEPTH) ** 0.25
BETA = (8.0 * DEPTH) ** -0.25
_SEG = (SSD_INNER, CONV_DIM, SSD_HEADS, ATT_WIDTH, ATT_WIDTH, ATT_WIDTH, N_BRANCH * D_MODEL)
IN_COLS = sum(_SEG)
SPLITS = tuple(int(v) for v in np.cumsum(_SEG)[:-1])

kernel_name = 'hybrid_ssd_bandattn_streaming_encoder'


def layer_norm(x, g, b):
    xf = x.astype(jnp.float32)
    mu = jnp.mean(xf, axis=-1, keepdims=True)
    var = jnp.mean(jnp.square(xf - mu), axis=-1, keepdims=True)
    y = (xf - mu) * lax.rsqrt(var + LN_EPS) * g.astype(jnp.float32) + b.astype(jnp.float32)
    return y.astype(x.dtype)


def causal_dwconv(u, hist, w, b):
    L = u.shape[1]
    up = jnp.concatenate([hist.astype(u.dtype), u], axis=1)
    y = b
    for tap in range(CONV_W):
        y = y + up[:, tap:tap + L] * w[tap]
    return y, up[:, L:]


def ssd_scan(x, dt, a, bm, cm, h0, q):
    b, l = x.shape[:2]
    c = l // q
    G, R, P, N = SSD_GROUPS, SSD_HPG, SSD_HEADDIM, SSD_STATE
    x = x.reshape(b, c, q, G, R, P)
    dt = dt.reshape(b, c, q, G, R)
    bm = bm.reshape(b, c, q, G, N)
    cm = cm.reshape(b, c, q, G, N)
    cs = jnp.cumsum(dt * a.reshape(G, R), axis=2)
    diff = cs[:, :, :, None] - cs[:, :, None, :]
    causal = jnp.tril(jnp.ones((q, q), dtype=bool))[:, :, None, None]
    decay_ij = jnp.exp(jnp.where(causal, diff, -jnp.inf))
    cb = jnp.einsum('bcign,bcjgn->bcijg', cm, bm)
    m = cb[..., None] * decay_ij * dt[:, :, None]
    y_diag = jnp.einsum('bcijgr,bcjgrp->bcigrp', m, x)
    decay_out = jnp.exp(cs[:, :, -1:] - cs)
    chunk_states = jnp.einsum('bcjgn,bcjgr,bcjgrp->bcgrpn', bm, decay_out * dt, x)
    chunk_decay = jnp.exp(cs[:, :, -1])

    def step(h, inp):
        s, d = inp
        return h * d[..., None, None] + s, h

    h_final, h_prev = lax.scan(step, h0.reshape(b, G, R, P, N),
                               (jnp.swapaxes(chunk_states, 0, 1), jnp.swapaxes(chunk_decay, 0, 1)))
    h_prev = jnp.swapaxes(h_prev, 0, 1)
    y_off = jnp.einsum('bcign,bcgrpn,bcigr->bcigrp', cm, h_prev, jnp.exp(cs))
    y = (y_diag + y_off).reshape(b, l, SSD_HEADS, P)
    return y, h_final.reshape(b, SSD_HEADS, P, N)


def ssd_branch(z, xbc, dt_raw, conv_hist, h0, conv_w, conv_b, dt_bias, a_log, d_skip, norm_w):
    bsz, L = z.shape[:2]
    xbc, conv_new = causal_dwconv(xbc, conv_hist, conv_w, conv_b)
    xbc = jax.nn.silu(xbc).astype(jnp.float32)
    xs, bm, cm = jnp.split(xbc, [SSD_INNER, SSD_INNER + SSD_GROUPS * SSD_STATE], axis=-1)
    xs = xs.reshape(bsz, L, SSD_HEADS, SSD_HEADDIM)
    bm = bm.reshape(bsz, L, SSD_GROUPS, SSD_STATE)
    cm = cm.reshape(bsz, L, SSD_GROUPS, SSD_STATE)
    dt = jax.nn.softplus(dt_raw.astype(jnp.float32) + dt_bias.astype(jnp.float32))
    a = -jnp.exp(a_log.astype(jnp.float32))
    y, h_new = ssd_scan(xs, dt, a, bm, cm, h0.astype(jnp.float32), min(CHUNK, L))
    y = (y + d_skip.astype(jnp.float32)[:, None] * xs).reshape(bsz, L, SSD_INNER)
    g = (y * jax.nn.silu(z.astype(jnp.float32))).reshape(bsz, L, SSD_GROUPS, SSD_INNER // SSD_GROUPS)
    g = g * lax.rsqrt(jnp.mean(jnp.square(g), axis=-1, keepdims=True) + RMS_EPS)
    out = g.reshape(bsz, L, SSD_INNER) * norm_w.astype(jnp.float32)
    return out.astype(z.dtype), conv_new, h_new.astype(h0.dtype)


def rel_position_bias(rel_bias, rel):
    return rel_bias[:, jnp.clip(rel, -REL_CLIP, REL_CLIP) + REL_CLIP].astype(jnp.float32)


def band_softmax_attention(q, kb, vb, bias, mask):
    s = jnp.einsum('bnqhd,bnkhd->bnhqk', q, kb).astype(jnp.float32) * (ATT_HEAD_DIM ** -0.5) + bias
    if mask is not None:
        s = jnp.where(mask[None, :, None], s, -jnp.inf)
    p = jax.nn.softmax(s, axis=-1).astype(vb.dtype)
    return jnp.einsum('bnhqk,bnkhd->bnqhd', p, vb)


def prompt_band_attention(q, k, v, rel_bias):
    b, S = q.shape[:2]
    nc = S // CHUNK
    pad = jnp.zeros((b, BAND_PAST, ATT_HEADS, ATT_HEAD_DIM), k.dtype)
    kp = jnp.concatenate([pad, k], axis=1)
    vp = jnp.concatenate([pad, v], axis=1)
    idx = (jnp.arange(nc) * CHUNK)[:, None] + jnp.arange(BAND_LEN)[None, :]
    kb = kp[:, idx]
    vb = vp[:, idx]
    mask = (idx >= BAND_PAST)[:, None, :]
    rel = jnp.arange(CHUNK)[:, None] - jnp.arange(BAND_LEN)[None, :] + BAND_PAST
    out = band_softmax_attention(q.reshape(b, nc, CHUNK, ATT_HEADS, ATT_HEAD_DIM), kb, vb,
                                 rel_position_bias(rel_bias, rel), mask)
    return out.reshape(b, S, ATT_HEADS, ATT_HEAD_DIM)


def sample_band_attention(q, k, v, k_hist, v_hist, rel_bias):
    L = q.shape[1]
    pc = k_hist.shape[1]
    kb = jnp.concatenate([k_hist.astype(k.dtype), k], axis=1)[:, None]
    vb = jnp.concatenate([v_hist.astype(v.dtype), v], axis=1)[:, None]
    rel = jnp.arange(L)[:, None] - jnp.arange(pc + L)[None, :] + pc
    out = band_softmax_attention(q[:, None], kb, vb, rel_position_bias(rel_bias, rel), None)
    return out[:, 0]


def trunk_layer(x, conv_hist, ssm_h0, k_hist, v_hist, w_in, conv_w, conv_b, dt_bias, a_log, d_skip,
                ssd_norm_w, rel_bias, w_ssd_out, w_att_out, w_o, ln1_g, ln1_b, w_gate_up, w_down,
                ln2_g, ln2_b):
    bsz, L = x.shape[:2]
    proj = x @ w_in
    z, xbc, dt_raw, q, k, v, gates = jnp.split(proj, SPLITS, axis=-1)
    ssd_y, conv_new, h_new = ssd_branch(z, xbc, dt_raw, conv_hist, ssm_h0, conv_w, conv_b,
                                        dt_bias, a_log, d_skip, ssd_norm_w)
    q = q.reshape(bsz, L, ATT_HEADS, ATT_HEAD_DIM)
    k = k.reshape(bsz, L, ATT_HEADS, ATT_HEAD_DIM)
    v = v.reshape(bsz, L, ATT_HEADS, ATT_HEAD_DIM)
    if k_hist is None:
        att = prompt_band_attention(q, k, v, rel_bias)
        keep = min(BAND_PAST, L)
        k_rows, v_rows = k[:, L - keep:], v[:, L - keep:]
    else:
        att = sample_band_attention(q, k, v, k_hist, v_hist, rel_bias)
        k_rows, v_rows = k, v
    att = att.reshape(bsz, L, ATT_WIDTH)
    g_ssd, g_att = jnp.split(jax.nn.sigmoid(gates), N_BRANCH, axis=-1)
    mixed = g_ssd * (ssd_y @ w_ssd_out) + g_att * (att @ w_att_out)
    h = layer_norm(ALPHA * x + mixed @ w_o, ln1_g, ln1_b)
    gate, up = jnp.split(h @ w_gate_up, 2, axis=-1)
    y = layer_norm(ALPHA * h + (jax.nn.silu(gate) * up) @ w_down, ln2_g, ln2_b)
    return y, conv_new, h_new, k_rows, v_rows


def setup_inputs(seed: int = 0) -> dict:
    key = jax.random.key(seed)
    ks = jax.random.split(key, 24)
    f32 = jnp.float32
    kv_rows = min(BAND_PAST, PAST_LEN)
    dt0 = jnp.exp(jax.random.uniform(ks[9], (DEPTH, SSD_HEADS), f32, math.log(1e-3), math.log(1e-1)))
    return {
        'x_prompt': jax.random.normal(ks[0], (BATCH, SEQ, D_MODEL), f32),
        'x_sample': jax.random.normal(ks[1], (DEC_BATCH, DEC_SEQ, D_MODEL), f32),
        'cache_k': jax.random.normal(ks[2], (DEPTH, DEC_BATCH, kv_rows, ATT_HEADS, ATT_HEAD_DIM), f32),
        'cache_v': jax.random.normal(ks[3], (DEPTH, DEC_BATCH, kv_rows, ATT_HEADS, ATT_HEAD_DIM), f32),
        'state_conv': jax.random.normal(ks[4], (DEPTH, DEC_BATCH, CONV_W - 1, CONV_DIM), f32),
        'state_ssm': 0.5 * jax.random.normal(ks[5], (DEPTH, DEC_BATCH, SSD_HEADS, SSD_HEADDIM, SSD_STATE), f32),
        'w_in': jax.random.normal(ks[6], (DEPTH, D_MODEL, IN_COLS), f32) * D_MODEL ** -0.5,
        'conv_w': jax.random.normal(ks[7], (DEPTH, CONV_W, CONV_DIM), f32) * CONV_W ** -0.5,
        'conv_b': 0.01 * jax.random.normal(ks[8], (DEPTH, CONV_DIM), f32),
        'dt_bias': dt0 + jnp.log(-jnp.expm1(-dt0)),
        'a_log': jnp.log(jax.random.uniform(ks[10], (DEPTH, SSD_HEADS), f32, 1.0, 16.0)),
        'd_skip': 1.0 + 0.1 * jax.random.normal(ks[11], (DEPTH, SSD_HEADS), f32),
        'ssd_norm_w': 1.0 + 0.05 * jax.random.normal(ks[12], (DEPTH, SSD_INNER), f32),
        'rel_bias': 0.5 * jax.random.normal(ks[13], (DEPTH, ATT_HEADS, N_REL), f32),
        'w_ssd_out': jax.random.normal(ks[14], (DEPTH, SSD_INNER, D_MODEL), f32) * SSD_INNER ** -0.5,
        'w_att_out': jax.random.normal(ks[15], (DEPTH, ATT_WIDTH, D_MODEL), f32) * ATT_WIDTH ** -0.5,
        'w_o': jax.random.normal(ks[16], (DEPTH, D_MODEL, D_MODEL), f32) * (BETA * D_MODEL ** -0.5),
        'ln1_g': 1.0 + 0.05 * jax.random.normal(ks[17], (DEPTH, D_MODEL), f32),
        'ln1_b': 0.01 * jax.random.normal(ks[18], (DEPTH, D_MODEL), f32),
        'w_gate_up': jax.random.normal(ks[19], (DEPTH, D_MODEL, 2 * D_FF), f32) * D_MODEL ** -0.5,
        'w_down': jax.random.normal(ks[20], (DEPTH, D_FF, D_MODEL), f32) * (BETA * D_FF ** -0.5),
        'ln2_g': 1.0 + 0.05 * jax.random.normal(ks[21], (DEPTH, D_MODEL), f32),
        'ln2_b': 0.01 * jax.random.normal(ks[22], (DEPTH, D_MODEL), f32),
    }


def reference(x_prompt, x_sample, cache_k, cache_v, state_conv, state_ssm, w_in, conv_w, conv_b,
              dt_bias, a_log, d_skip, ssd_norm_w, rel_bias, w_ssd_out, w_att_out, w_o, ln1_g, ln1_b,
              w_gate_up, w_down, ln2_g, ln2_b):
    yp, ys = x_prompt, x_sample
    conv_p, ssm_p, k_p, v_p = [], [], [], []
    conv_s, ssm_s, k_s, v_s = [], [], [], []
    for layer in range(DEPTH):
        params = (w_in[layer], conv_w[layer], conv_b[layer], dt_bias[layer], a_log[layer], d_skip[layer],
                  ssd_norm_w[layer], rel_bias[layer], w_ssd_out[layer], w_att_out[layer], w_o[layer],
                  ln1_g[layer], ln1_b[layer], w_gate_up[layer], w_down[layer], ln2_g[layer], ln2_b[layer])
        zero_conv = jnp.zeros((yp.shape[0], CONV_W - 1, CONV_DIM), yp.dtype)
        zero_ssm = jnp.zeros((yp.shape[0], SSD_HEADS, SSD_HEADDIM, SSD_STATE), yp.dtype)
        yp, c, h, k, v = trunk_layer(yp, zero_conv, zero_ssm, None, None, *params)
        conv_p.append(c); ssm_p.append(h); k_p.append(k); v_p.append(v)
        ys, c, h, k, v = trunk_layer(ys, state_conv[layer], state_ssm[layer], cache_k[layer],
                                     cache_v[layer], *params)
        conv_s.append(c); ssm_s.append(h); k_s.append(k); v_s.append(v)
    return (yp, ys, jnp.stack(conv_p), jnp.stack(ssm_p), jnp.stack(k_p), jnp.stack(v_p),
            jnp.stack(conv_s), jnp.stack(ssm_s), jnp.stack(k_s), jnp.stack(v_s))
```

```python
import numpy as np
from contextlib import ExitStack
import concourse.bass as bass
import concourse.mybir as mybir
from concourse.bass_utils import run_bass_kernel_spmd

F32 = mybir.dt.float32
BF16 = mybir.dt.bfloat16
AF = mybir.ActivationFunctionType
ALU = mybir.AluOpType
AX = mybir.AxisListType

D = 2048
NC_ = 8
NPR = 1024
NOWN = 1152
NHALO = 512
INC = 20544
DFF = 5632
C_Z, C_X, C_B, C_C, C_DT, C_Q, C_K, C_V, C_GS, C_GA = 0, 4096, 8192, 9216, 10240, 10304, 12352, 14400, 16448, 18496
ALPHA = 2.0 ** 0.25
NL = 10
NPRE = 7
CINW = 1161
DO_PRE = True
DEBUG_ALLOC = False
DO_Y = True
STOP = 99
P1_STOP = 99
DEBUG_CORES = None
DEBUG_TRACE = False
LAST_EXEC_NS = [None]
NPRE_RUN = NPRE


class T:
    def __init__(self, h, excl=False):
        self.h = h
        self.w = None
        self.r = {}
        self.excl = excl

    def __getitem__(self, k):
        return self.h[k]


class TV:
    def __init__(self, base, h):
        object.__setattr__(self, "base", base)
        object.__setattr__(self, "h", h)

    def __getitem__(self, k):
        return self.h[k]

    def __getattr__(self, n):
        return getattr(self.base, n)

    def __setattr__(self, n, v):
        setattr(self.base, n, v)


class Sched:
    def __init__(self, nc, es):
        self.nc = nc
        self.es = es
        self.E = {}
        for name, h in (("pe", nc.tensor), ("act", nc.scalar), ("dve", nc.vector), ("pool", nc.gpsimd), ("sp", nc.sync)):
            sem = es.enter_context(nc.semaphore("e_" + name))
            self.E[name] = dict(h=h, sem=sem, cnt=0, waited={}, name=name)
        self.lanes = {}
        self.lptr = {}
        for q in ("sp", "pool"):
            self.lanes[q] = [dict(sem=es.enter_context(nc.semaphore(f"l_{q}{i}")), val=0, idx=i) for i in range(NL)]
            self.lptr[q] = 0
        self.nsb = 0

    def sb(self, shape, dt, es=None, name=None):
        self.nsb += 1
        h = (es or self.es).enter_context(self.nc.sbuf_tensor(f"{name or "sb"}_{self.nsb}", list(shape), dt))
        if DEBUG_ALLOC:
            print('alloc', name, shape, dt, 'remaining', self.nc.sbuf_bytes_remaining)
        return T(h)

    def _wait(self, E, evs):
        best = {}
        for ev in evs:
            k = id(ev[0])
            if k not in best or best[k][1] < ev[1]:
                best[k] = ev
        for k, ev in best.items():
            if E["waited"].get(k, 0) < ev[1]:
                E["h"].wait_ge(ev[0], ev[1])
                E["waited"][k] = ev[1]

    def _deps(self, eng, reads, writes):
        evs = []
        for t in reads:
            if t.w is not None and not (eng == "pe" and t.w[2] == "pe"):
                evs.append(t.w)
            if t.excl:
                for k, ev in t.r.items():
                    if k != eng:
                        evs.append(ev)
        for t in writes:
            if t.w is not None and not (eng == "pe" and t.w[2] == "pe"):
                evs.append(t.w)
            for ev in t.r.values():
                if not (eng == "pe" and ev[2] == "pe"):
                    evs.append(ev)
        return evs

    def op(self, eng, fn, reads=(), writes=(), sig=True):
        E = self.E[eng]
        self._wait(E, self._deps(eng, reads, writes))
        ins = fn(E["h"])
        if sig:
            E["cnt"] += 1
            ins.then_inc(E["sem"], 1)
            ev = (E["sem"], E["cnt"], eng)
        else:
            ev = (E["sem"], E["cnt"] + 1, eng)
        for t in reads:
            t.r[eng] = ev
        for t in writes:
            t.w = ev
            t.r = {}
        return ins

    def dma(self, q, out, in_, reads=(), writes=(), **kw):
        E = self.E[q]
        ln = self.lanes[q][self.lptr[q] % NL]
        self.lptr[q] += 1
        evs = self._deps(q, reads, writes)
        if ln["val"] > 0:
            evs.append((ln["sem"], ln["val"], "dma"))
        self._wait(E, evs)
        ins = E["h"].dma_start(out=out, in_=in_, **kw)
        ln["val"] += 16
        ins.then_inc(ln["sem"], 16)
        ev = (ln["sem"], ln["val"], "dma")
        for t in reads:
            t.r[(q, ln["idx"])] = ev
        for t in writes:
            t.w = ev
            t.r = {}

    def barrier(self, engs=("pe", "act", "dve", "pool", "sp")):
        evs = [(e["sem"], e["cnt"], n) for n, e in self.E.items() if e["cnt"] > 0]
        for q in self.lanes:
            for ln in self.lanes[q]:
                if ln["val"] > 0:
                    evs.append((ln["sem"], ln["val"], "dma"))
        for n in engs:
            self._wait(self.E[n], evs)


def build_program():
    nc = bass.Bass("TRN2", target_bir_lowering=False)

    def din(name, shape):
        return nc.dram_tensor(name, list(shape), F32, kind="ExternalInput").ap()

    def dout(name, shape):
        return nc.dram_tensor(name, list(shape), F32, kind="ExternalOutput").ap()

    xo = din("xo", [NOWN, D]); xh = din("xh", [NHALO, D]); xp = din("xp", [NPRE * NPR, D])
    ck = din("ck", [2, 512, D]); cv = din("cv", [2, 512, D])
    sconv = din("sconv", [2, 3, 6144]); sssm = din("sssm", [2, 4096, 128])
    w_in = din("w_in", [D, INC]); conv_w = din("conv_w", [4, 6144]); conv_b = din("conv_b", [6144])
    dt_bias = din("dt_bias", [64]); a_log = din("a_log", [64]); d_skip = din("d_skip", [64])
    norm_w = din("norm_w", [4096]); bt_in = din("bt", [3, 128, 16 * 128]); relc = din("relc", [16])
    w_so = din("w_so", [4096, D]); w_ao = din("w_ao", [D, D]); w_o = din("w_o", [D, D])
    ln1_g = din("ln1_g", [D]); ln1_b = din("ln1_b", [D]); ln2_g = din("ln2_g", [D]); ln2_b = din("ln2_b", [D])
    w_gu = din("w_gu", [D, 2 * DFF]); w_dn = din("w_dn", [DFF, D])
    cst = din("cst", [128, 512])
    hones_in = din("hones", [128, 128]); pmask_in = din("pmask", [64, NPRE])

    y_o = dout("y", [NOWN, D]); convp_o = dout("convp", [3, 6144]); ssmp_o = dout("ssmp", [4096, 128])
    kp_o = dout("kp", [512, D]); vp_o = dout("vp", [512, D])
    convs_o = dout("convs", [2, 3, 6144]); ssms_o = dout("ssms", [2, 4096, 128])
    ks_o = dout("ks", [128, D]); vs_o = dout("vs", [128, D])
    att_scr = nc.dram_tensor("att_scr", [128, 16, NOWN], BF16).ap()
    ssdy_scr = nc.dram_tensor("ssdy_scr", [8, 128, 4, NOWN], BF16).ap()
    ATTS = T(None)
    SSDYS = T(None)

    with ExitStack() as es:
        S = Sched(nc, es)
        psf = [T(es.enter_context(nc.psum_tensor(f"psf{i}", [128, 512], F32)), excl=True) for i in range(8)]
        psbv = [TV(t, t.h.bitcast(BF16)) for t in psf]
        pctr = [0, 0]
        pf_set = [[0, 1, 2, 3, 4, 5]]
        pb_set = [[6, 7]]

        def PF():
            pctr[0] += 1
            s_ = pf_set[0]
            return psf[s_[pctr[0] % len(s_)]]

        def PB():
            pctr[1] += 1
            s_ = pb_set[0]
            return psbv[s_[pctr[1] % len(s_)]]

        CF = S.sb([128, 512], F32)
        S.dma("sp", CF[:], cst[:, :], writes=[CF])
        identf = CF[:, 0:128]
        mle = CF[0:64, 128:192]
        lt = CF[0:64, 192:256]
        onesf = CF[0:64, 256:384]
        CB = S.sb([128, 384], BF16)
        S.dma("pool", CB[:, 0:256], cst[:, 0:256], writes=[CB])
        S.dma("pool", CB[:, 256:384], hones_in[:, :], writes=[CB])
        OB = S.sb([128, 128], BF16)
        S.op("dve", lambda e: e.memset(OB[:], 1.0), writes=[OB])
        identb = CB[:, 0:128]
        honesb = CB[:, 256:384]
        PM = S.sb([64, NPRE], F32)
        S.dma("sp", PM[:], pmask_in[:, :], writes=[PM])
        HV = S.sb([128, 4, 64], F32)
        S.dma("sp", HV[:, 0, :], dt_bias.partition_broadcast(128), writes=[HV])
        S.dma("sp", HV[:, 1, :], a_log.partition_broadcast(128), writes=[HV])
        S.dma("sp", HV[:, 2, :], d_skip.partition_broadcast(128), writes=[HV])
        S.op("act", lambda e: e.activation(out=HV[:, 1, :], in_=HV[:, 1, :], func=AF.Exp), reads=[HV], writes=[HV])
        S.op("dve", lambda e: e.tensor_scalar(out=HV[:, 1, :], in0=HV[:, 1, :], scalar1=-1.0, scalar2=None, op0=ALU.mult),
             reads=[HV], writes=[HV])
        CW = S.sb([128, 48, 4], F32)
        CBI = S.sb([128, 48], F32)
        with nc.allow_non_contiguous_dma(reason="small param layout"):
            for tap in range(4):
                S.dma("sp", CW[:, :, tap], conv_w[tap].rearrange("(t p) -> p t", p=128), writes=[CW])
            S.dma("sp", CBI[:, :], conv_b.rearrange("(t p) -> p t", p=128), writes=[CBI])
        SH = S.sb([128, 48, 2, 3], F32)
        with nc.allow_non_contiguous_dma(reason="conv state transpose load"):
            for s in range(2):
                for r in range(3):
                    S.dma("sp", SH[:, :, s, r], sconv[s, r].rearrange("(t p) -> p t", p=128), writes=[SH])

        NSLOT = 3
        WS = [S.sb([128, 16, 512], BF16, name=f"ws{i}") for i in range(NSLOT)]
        wctr = [0]

        def load_w(src, kt, ncol):
            wctr[0] += 1
            sl = WS[wctr[0] % NSLOT]
            S.dma("pool", sl[:, 0:kt, 0:ncol], src.rearrange("(k p) c -> p k c", p=128), writes=[sl])
            return sl

        XB = [None, None]
        xbc = [0]

        def make_xT(src_rows, dst, col0):
            xbc[0] += 1
            xb = XB[xbc[0] % 2]
            S.dma("pool", xb[:], src_rows, writes=[xb])
            for half in range(2):
                pb = PB()
                for j in range(8):
                    k = half * 8 + j
                    S.op("pe", lambda e, k=k, j=j, pb=pb: e.transpose(pb[:, j * 128:(j + 1) * 128], xb[:, k * 128:(k + 1) * 128], identb),
                         reads=[xb], writes=[pb], sig=(j == 7))
                eng = "act" if half == 0 else "dve"
                if eng == "act":
                    S.op("act", lambda e, pb=pb, half=half: e.copy(out=dst[:, half * 8:(half + 1) * 8, col0:col0 + 128],
                                                                     in_=pb[:, :].rearrange("p (k t) -> p k t", k=8)),
                         reads=[pb], writes=[dst])
                else:
                    S.op("dve", lambda e, pb=pb, half=half: e.tensor_copy(out=dst[:, half * 8:(half + 1) * 8, col0:col0 + 128],
                                                                            in_=pb[:, :].rearrange("p (k t) -> p k t", k=8)),
                         reads=[pb], writes=[dst])


        hcs = nc.dram_tensor("hcscratch", [128, 4096], F32).ap()
        HCS = T(None)

        def dt_chain(ps, dst_dt, dst_da, mask_ap=None):
            pass

        if STOP == 0:
            S.barrier()
            return nc
        if DO_PRE:
            with ExitStack() as es1:
                XB[0] = S.sb([128, D], BF16, es1, name="xb0p"); XB[1] = S.sb([128, D], BF16, es1, name="xb1p")
                HC = S.sb([128, 4096], F32, es1, name="hc")
                S.op("dve", lambda e: e.memset(HC[:], 0.0), writes=[HC])
                XTPs = [S.sb([128, 16, NPR], BF16, es1, name=f"xtp{i}") for i in range(1)]
                CINP = [S.sb([128, 3 + NPR], F32, es1, name=f"cinp{i}") for i in range(2)]
                HIST = S.sb([128, 40, 3], F32, es1, name="hist")
                S.op("dve", lambda e: e.memset(HIST[:], 0.0), writes=[HIST])
                XSBP = S.sb([128, 8, NPR], BF16, es1, name="xsbp")
                XSG = [S.sb([128, 4, NPR], BF16, es1, name=f"xsg{i}") for i in range(2)]
                ACC = [S.sb([128, NPR], F32, es1, name=f"accp{i}") for i in range(2)]
                DTPs = [S.sb([64, 16, 64], F32, es1, name=f"dtp{i}") for i in range(2)]
                DAPs = [S.sb([64, 16, 64], F32, es1, name=f"dap{i}") for i in range(2)]
                WDPs = [S.sb([64, 16, 64], F32, es1, name=f"wdp{i}") for i in range(2)]
                CDPs = [S.sb([128, 64], F32, es1, name=f"cdp{i}") for i in range(2)]
                TMP = [S.sb([64, 64], F32, es1, name=f"tmpp{i}") for i in range(2)]
                XWP = [S.sb([64, 512], BF16, es1, name=f"xwp{i}") for i in range(3)]
                BTP = [S.sb([64, 128], BF16, es1, name=f"btp{i}") for i in range(3)]
                HTMP = [S.sb([128, 512], F32, es1, name=f"htmp{i}") for i in range(2)]
                pf_set[0] = [0, 1, 2]
                PSACC = psf[3]
                pb_set[0] = [4, 5, 6, 7]

                def interleave(g1, g2, n1=1, n2=1):
                    a1 = a2 = True
                    while a1 or a2:
                        for _ in range(n1):
                            if a1:
                                try:
                                    next(g1)
                                except StopIteration:
                                    a1 = False
                        for _ in range(n2):
                            if a2:
                                try:
                                    next(g2)
                                except StopIteration:
                                    a2 = False
                        yield

                def run(g):
                    for _ in g:
                        pass

                def gen_A(blk):
                    XTP = XTPs[0]
                    DTP, DAP, WDP, CDP = DTPs[blk % 2], DAPs[blk % 2], WDPs[blk % 2], CDPs[blk % 2]
                    for i in range(8):
                        make_xT(xp[blk * NPR + i * 128: blk * NPR + (i + 1) * 128, :], XTP, i * 128)
                        yield
                    wdt = load_w(w_in[:, C_DT:C_DT + 64], 16, 64)
                    for c in range(16):
                        ps = PF()
                        for k in range(16):
                            S.op("pe", lambda e, k=k, c=c, ps=ps: e.matmul(ps[0:64, 0:64], lhsT=XTP[:, k, c * 64:(c + 1) * 64], rhs=wdt[:, k, 0:64],
                                                                           start=(k == 0), stop=(k == 15)),
                                 reads=[XTP, wdt], writes=[ps], sig=(k == 15))
                        tm = TMP[c % 2]
                        S.op("dve", lambda e, ps=ps, tm=tm: e.tensor_tensor(out=tm[:], in0=ps[0:64, 0:64], in1=HV[0:64, 0, :], op=ALU.add),
                             reads=[ps, HV], writes=[tm])
                        S.op("act", lambda e, tm=tm: e.activation(out=tm[:], in_=tm[:], func=AF.Exp), reads=[tm], writes=[tm])
                        S.op("act", lambda e, tm=tm, c=c: e.activation(out=DTP[:, c, :], in_=tm[:], func=AF.Ln, bias=1.0, scale=1.0),
                             reads=[tm], writes=[DTP])
                        S.op("dve", lambda e, c=c: e.tensor_scalar(out=DTP[:, c, :], in0=DTP[:, c, :], scalar1=PM[:, blk:blk + 1], scalar2=None,
                                                                   op0=ALU.mult), reads=[DTP, PM], writes=[DTP])
                        S.op("dve", lambda e, c=c: e.tensor_tensor(out=DAP[:, c, :], in0=DTP[:, c, :], in1=HV[0:64, 1, :], op=ALU.mult),
                             reads=[DTP, HV], writes=[DAP])
                        yield
                    for c in range(16):
                        ps2 = PF()
                        S.op("pe", lambda e, c=c, ps2=ps2: e.matmul(ps2[0:64, 0:64], lhsT=lt, rhs=DAP[:, c, :], start=True, stop=(c == 15)),
                             reads=[DAP, CF], writes=[ps2], sig=(c == 15))
                        for c2 in range(c + 1, 16):
                            S.op("pe", lambda e, c2=c2, ps2=ps2: e.matmul(ps2[0:64, 0:64], lhsT=CF[0:64, 256:320], rhs=DAP[:, c2, :], start=False, stop=(c2 == 15)),
                                 reads=[DAP, CF], writes=[ps2], sig=(c2 == 15))
                        S.op("act", lambda e, c=c, ps2=ps2: e.activation(out=WDP[:, c, :], in_=ps2[0:64, 0:64], func=AF.Exp), reads=[ps2], writes=[WDP])
                        S.op("dve", lambda e, c=c: e.tensor_tensor(out=WDP[:, c, :], in0=WDP[:, c, :], in1=DTP[:, c, :], op=ALU.mult),
                             reads=[WDP, DTP], writes=[WDP])
                        yield
                    ps3 = PF()
                    for c in range(16):
                        S.op("pe", lambda e, c=c, ps3=ps3: e.matmul(ps3[:, 0:64], lhsT=onesf, rhs=DAP[:, c, :], start=(c == 0), stop=(c == 15)),
                             reads=[DAP, CF], writes=[ps3], sig=(c == 15))
                    S.op("act", lambda e, ps3=ps3: e.activation(out=CDP[:, :], in_=ps3[:, 0:64], func=AF.Exp), reads=[ps3], writes=[CDP])
                    yield

                def proj_tile(blk, wb, ct, wsl):
                    XTP = XTPs[0]
                    xsg = XSG[wb % 2]
                    ft = wb * 4 + ct
                    cin = CINP[ft % 2]
                    S.op("act", lambda e: e.copy(out=cin[:, 0:3], in_=HIST[:, ft, :]), reads=[HIST], writes=[cin])
                    for tg in range(2):
                        ps = PF()
                        for k in range(16):
                            S.op("pe", lambda e, k=k, ps=ps, tg=tg: e.matmul(ps[:, 0:512], lhsT=wsl[:, k, ct * 128:(ct + 1) * 128],
                                                                          rhs=XTP[:, k, tg * 512:(tg + 1) * 512], start=(k == 0), stop=(k == 15)),
                                 reads=[wsl, XTP], writes=[ps], sig=(k == 15))
                        S.op("act", lambda e, ps=ps, tg=tg: e.copy(out=cin[:, 3 + tg * 512:3 + (tg + 1) * 512], in_=ps[:, 0:512]),
                             reads=[ps], writes=[cin])
                    S.op("act", lambda e: e.copy(out=HIST[:, ft, :], in_=cin[:, NPR:NPR + 3]), reads=[cin], writes=[HIST])
                    acc = ACC[ft % 2]
                    S.op("dve", lambda e: e.tensor_scalar(out=acc[:], in0=cin[:, 0:NPR], scalar1=CW[:, ft, 0:1], scalar2=CBI[:, ft:ft + 1], op0=ALU.mult, op1=ALU.add),
                         reads=[cin, CW, CBI], writes=[acc])
                    for tap in range(1, 4):
                        S.op("dve", lambda e, tap=tap: e.scalar_tensor_tensor(
                            out=acc[:], in0=cin[:, tap:tap + NPR], scalar=CW[:, ft, tap:tap + 1], in1=acc[:], op0=ALU.mult, op1=ALU.add),
                            reads=[cin, CW, acc], writes=[acc])
                    xdst, xdi = (XSBP, ft - 32) if wb >= 8 else (xsg, ct)
                    S.op("act", lambda e: e.activation(out=xdst[:, xdi, :], in_=acc[:], func=AF.Silu), reads=[acc], writes=[xdst])

                NXW = 8
                XWP8 = [S.sb([64, 512], BF16, es1, name=f"xwq{i}") for i in range(NXW)]
                BTP8 = [S.sb([64, 128], BF16, es1, name=f"btq{i}") for i in range(NXW)]

                def chunk_T(blk, g, c):
                    WDP = WDPs[blk % 2]
                    xsg = XSG[g % 2]
                    pb = PB()
                    for j in range(4):
                        S.op("pe", lambda e, j=j: e.transpose(pb[0:64, j * 128:(j + 1) * 128], xsg[:, j, c * 64:(c + 1) * 64], identb),
                             reads=[xsg], writes=[pb], sig=False)
                    S.op("pe", lambda e: e.transpose(pb[0:64, 512:640], XSBP[:, g, c * 64:(c + 1) * 64], identb), reads=[XSBP], writes=[pb])
                    xw = XWP8[c % NXW]
                    bt_ = BTP8[c % NXW]
                    S.op("dve", lambda e: e.tensor_tensor(
                        out=xw[:, :].rearrange("p (h q) -> p h q", h=8), in0=pb[0:64, 0:512].rearrange("p (h q) -> p h q", h=8),
                        in1=WDP[:, c, g * 8:(g + 1) * 8].rearrange("p (h o) -> p h o", o=1).to_broadcast([64, 8, 64]), op=ALU.mult),
                        reads=[pb, WDP], writes=[xw])
                    S.op("act", lambda e: e.copy(out=bt_[:], in_=pb[0:64, 512:640]), reads=[pb], writes=[bt_])

                def chunk_M(blk, g, c):
                    CDP = CDPs[blk % 2]
                    xw = XWP8[c % NXW]
                    bt_ = BTP8[c % NXW]
                    S.op("pe", lambda e: e.matmul(PSACC[:, 0:512], lhsT=bt_[:], rhs=xw[:], start=(c == 0), stop=(c == 15)),
                         reads=[xw, bt_], writes=[PSACC])
                    if c == 15:
                        ht = HTMP[g % 2]
                        S.op("dve", lambda e: e.tensor_tensor(
                            out=ht[:, :].rearrange("p (h q) -> p h q", h=8), in0=HC[:, g * 512:(g + 1) * 512].rearrange("p (h q) -> p h q", h=8),
                            in1=CDP[:, g * 8:(g + 1) * 8].rearrange("p (h o) -> p h o", o=1).to_broadcast([128, 8, 64]), op=ALU.mult),
                            reads=[HC, CDP], writes=[ht])
                        S.op("dve", lambda e: e.tensor_tensor(out=HC[:, g * 512:(g + 1) * 512], in0=ht[:], in1=PSACC[:, 0:512], op=ALU.add),
                             reads=[ht, PSACC], writes=[HC])

                def do_B(blk):
                    for wb in (8, 9, 0):
                        wsl = load_w(w_in[:, C_X + wb * 512:C_X + (wb + 1) * 512], 16, 512)
                        for ct in range(4):
                            proj_tile(blk, wb, ct, wsl)
                    for g in range(1, 8):
                        wsl = load_w(w_in[:, C_X + g * 512:C_X + (g + 1) * 512], 16, 512)
                        for q in range(4):
                            for c in range(4 * q, 4 * q + 4):
                                chunk_T(blk, g - 1, c)
                            proj_tile(blk, g, q, wsl)
                            for c in range(4 * q, 4 * q + 4):
                                chunk_M(blk, g - 1, c)
                    nxt = gen_A(blk + 1) if blk + 1 < NPRE_RUN else iter(())
                    for q in range(4):
                        for c in range(4 * q, 4 * q + 4):
                            chunk_T(blk, 7, c)
                        for _ in range(11):
                            next(nxt, None)
                        for c in range(4 * q, 4 * q + 4):
                            chunk_M(blk, 7, c)
                    run(nxt)

                run(gen_A(0))
                for blk in range(NPRE_RUN):
                    do_B(blk)
                S.dma("sp", hcs[:, :], HC[:], reads=[HC], writes=[HCS])
                S.barrier()
                pf_set[0] = [0, 1, 2, 3, 4, 5]
                pb_set[0] = [6, 7]

        if STOP == 1:
            S.barrier()
            return nc
        XH3 = S.sb([128, 16, 3], BF16, name="xh3")
        esxo = ExitStack()
        XTO = S.sb([128, 16, NOWN], BF16, esxo, name="xto")
        esa = ExitStack()
        ATT = S.sb([128, 16, NOWN], BF16, esa, name="att")
        esh = ExitStack()
        XTH = S.sb([128, 16, NHALO], BF16, esh, name="xth")
        with ExitStack() as esx:
            XB[0] = S.sb([128, D], BF16, esx, name="xb0m"); XB[1] = S.sb([128, D], BF16, esx, name="xb1m")
            for i in range(4):
                make_xT(xh[i * 128:(i + 1) * 128, :], XTH, i * 128)
            for i in range(9):
                make_xT(xo[i * 128:(i + 1) * 128, :], XTO, i * 128)
            S.barrier()
        if STOP == 2:
            S.barrier()
            return nc
        with ExitStack() as es2:
            BT = S.sb([128, 3, 4, 128], F32, es2, name="btb")
            RC = S.sb([128, 16], F32, es2, name="rc")
            S.dma("sp", RC[:], relc.partition_broadcast(128), writes=[RC])
            QT = S.sb([128, 4, NOWN], BF16, es2, name="qt")
            KT = S.sb([128, 4, 13 * 128], BF16, es2, name="kt")
            KTC = S.sb([128, 4, 512], BF16, es2, name="ktc")
            VB = S.sb([128, 13, 512], BF16, es2, name="vb")
            CVB = S.sb([128, 4, 512], BF16, es2, name="cvb")
            CKB = [S.sb([128, 512], BF16, es2, name=f"ckb{i}") for i in range(2)]
            KB = [S.sb([128, 512], BF16, es2, name=f"kb{i}") for i in range(2)]
            KF = [S.sb([128, 512], F32, es2, name=f"kf{i}") for i in range(2)]
            PT = [S.sb([128, 5, 64], BF16, es2, name=f"pt{i}") for i in range(2)]
            SB_ = [S.sb([128, 128], F32, es2, name=f"sbias{i}") for i in range(2)]
            RD = [S.sb([128, 64], F32, es2, name=f"rd{i}") for i in range(2)]
            kfc = [0]
            for hq in range(4):
                for v in range(3):
                    S.dma("sp", BT[:, v, :, :], bt_in[v].rearrange("p (h x) -> p h x", h=16)[:, hq * 4:(hq + 1) * 4, :], writes=[BT])
                for v in range(3):
                    S.op("dve", lambda e, v=v, hq=hq: e.tensor_tensor(out=BT[:, v, :, :], in0=BT[:, v, :, :],
                                                               in1=RC[:, hq * 4:(hq + 1) * 4].rearrange("p (h o) -> p h o", o=1).to_broadcast([128, 4, 128]), op=ALU.subtract),
                         reads=[BT, RC], writes=[BT])
                wq = load_w(w_in[:, C_Q + hq * 512:C_Q + (hq + 1) * 512], 16, 512)
                wk = load_w(w_in[:, C_K + hq * 512:C_K + (hq + 1) * 512], 16, 512)
                wv = load_w(w_in[:, C_V + hq * 512:C_V + (hq + 1) * 512], 16, 512)
                for hh in range(4):
                    for tg in range(3):
                        ps = PF()
                        for k in range(16):
                            S.op("pe", lambda e, k=k, ps=ps, hh=hh, tg=tg: e.matmul(ps[:, 0:384], lhsT=wq[:, k, hh * 128:(hh + 1) * 128],
                                                                                  rhs=XTO[:, k, tg * 384:(tg + 1) * 384], start=(k == 0), stop=(k == 15)),
                                 reads=[wq, XTO], writes=[ps], sig=(k == 15))
                        S.op("act", lambda e, ps=ps, hh=hh, tg=tg: e.activation(out=QT[:, hh, tg * 384:(tg + 1) * 384], in_=ps[:, 0:384], func=AF.Copy,
                                                                              scale=float(128 ** -0.5)), reads=[ps], writes=[QT])
                for a in range(13):
                    src, c0 = (XTH, a * 128) if a < 4 else (XTO, (a - 4) * 128)
                    for which, wsl in (("k", wk), ("v", wv)):
                        ps = PF()
                        for k in range(16):
                            S.op("pe", lambda e, k=k, ps=ps, src=src, c0=c0, wsl=wsl: e.matmul(ps[:, 0:512], lhsT=src[:, k, c0:c0 + 128], rhs=wsl[:, k, 0:512],
                                                                                             start=(k == 0), stop=(k == 15)),
                                 reads=[src, wsl], writes=[ps], sig=(k == 15))
                        if a >= 8:
                            kfc[0] += 1
                            kf = KF[kfc[0] % 2]
                            S.op("dve", lambda e, ps=ps, kf=kf: e.tensor_copy(out=kf[:], in_=ps[:, 0:512]), reads=[ps], writes=[kf])
                            if a < 12:
                                dst = (kp_o if which == "k" else vp_o)[(a - 8) * 128:(a - 7) * 128, hq * 512:(hq + 1) * 512]
                            else:
                                dst = (ks_o if which == "k" else vs_o)[:, hq * 512:(hq + 1) * 512]
                            S.dma("sp", dst, kf[:], reads=[kf])
                        if which == "v":
                            S.op("act", lambda e, ps=ps, a=a: e.copy(out=VB[:, a, :], in_=ps[:, 0:512]), reads=[ps], writes=[VB])
                        else:
                            kb = KB[a % 2]
                            S.op("act", lambda e, ps=ps, kb=kb: e.copy(out=kb[:], in_=ps[:, 0:512]), reads=[ps], writes=[kb])
                            pb = PB()
                            for hh in range(4):
                                S.op("pe", lambda e, pb=pb, kb=kb, hh=hh: e.transpose(pb[:, hh * 128:(hh + 1) * 128], kb[:, hh * 128:(hh + 1) * 128], identb),
                                     reads=[kb], writes=[pb], sig=(hh == 3))
                            S.op("dve", lambda e, pb=pb, a=a: e.tensor_copy(out=KT[:, :, a * 128:(a + 1) * 128],
                                                                            in_=pb[:, 0:512].rearrange("p (h t) -> p h t", h=4)), reads=[pb], writes=[KT])
                def prep_cache(s):
                    S.dma("pool", CVB[:, :, :], cv[s, :, hq * 512:(hq + 1) * 512].rearrange("(t p) c -> p t c", p=128), writes=[CVB])
                    for t in range(4):
                        ckb = CKB[t % 2]
                        S.dma("pool", ckb[:], ck[s, t * 128:(t + 1) * 128, hq * 512:(hq + 1) * 512], writes=[ckb])
                        pb = PB()
                        for hh in range(4):
                            S.op("pe", lambda e, pb=pb, ckb=ckb, hh=hh: e.transpose(pb[:, hh * 128:(hh + 1) * 128], ckb[:, hh * 128:(hh + 1) * 128], identb),
                                 reads=[ckb], writes=[pb], sig=(hh == 3))
                        S.op("dve", lambda e, pb=pb, s=s, t=t: e.tensor_copy(out=KTC[:, :, t * 128:(t + 1) * 128],
                                                                             in_=pb[:, 0:512].rearrange("p (h t) -> p h t", h=4)), reads=[pb], writes=[KTC])
                order = [(hh, c) for hh in range(4) for c in range(16)] + [(hh, 16 + s) for s in range(2) for hh in range(4)]

                def stage_S(n_it, hh, c):
                    pf_set[0] = [0, 1]
                    tiles = []
                    if c < 16:
                        par = c % 2
                        var = par
                        a_lo = c // 2
                        for t in range(5):
                            a = a_lo + t
                            r0, r1 = 0, 128
                            if par == 0 and t == 4:
                                r1 = 64
                            if par == 1 and t == 0:
                                r0 = 64
                            on = honesb if a < 4 else OB[:, :]
                            tiles.append((KT[:, hh, a * 128:(a + 1) * 128], VB[:, a, hh * 128:(hh + 1) * 128], r0, r1, on, [KT, VB]))
                        q0 = c * 64
                    else:
                        s = c - 16
                        var = 0 if s == 0 else 2
                        for t in range(4):
                            tiles.append((KTC[:, hh, t * 128:(t + 1) * 128], CVB[:, t, hh * 128:(hh + 1) * 128], 0, 128, OB[:, :], [KTC, CVB]))
                        tiles.append((KT[:, hh, 12 * 128:13 * 128], VB[:, 12, hh * 128:(hh + 1) * 128], 64 * s, 64 * s + 64, OB[:, :], [KT, VB]))
                        q0 = NPR + 64 * s
                    ps = PF()
                    for t, (kap, vap, r0, r1, on, rd) in enumerate(tiles):
                        S.op("pe", lambda e, kap=kap, t=t: e.matmul(ps[:, t * 64:(t + 1) * 64], lhsT=kap, rhs=QT[:, hh, q0:q0 + 64], start=True, stop=True),
                             reads=rd + [QT], writes=[ps], sig=(t == 4))
                    pt = PT[n_it % 2]
                    sbias = SB_[n_it % 2]
                    S.op("act", lambda e: e.activation(out=pt[:, 0:3, :], in_=ps[:, 0:192].rearrange("p (t q) -> p t q", t=3), func=AF.Exp), reads=[ps], writes=[pt])
                    S.op("dve", lambda e: e.tensor_tensor(out=sbias[:], in0=ps[:, 192:320], in1=BT[:, var, hh, :], op=ALU.add), reads=[ps, BT], writes=[sbias])
                    S.op("act", lambda e: e.activation(out=pt[:, 3:5, :], in_=sbias[:, :].rearrange("p (t q) -> p t q", t=2), func=AF.Exp), reads=[sbias], writes=[pt])
                    return (n_it, hh, tiles, q0, pt)

                def stage_V(ctx):
                    n_it, hh, tiles, q0, pt = ctx
                    hg = hq * 4 + hh
                    pf_set[0] = [2, 3, 4, 5]
                    po = PF()
                    pd = PF()
                    for t, (kap, vap, r0, r1, on, rd) in enumerate(tiles):
                        S.op("pe", lambda e, vap=vap, t=t, r0=r0, r1=r1: e.matmul(po[:, 0:64], lhsT=vap[r0:r1, :], rhs=pt[r0:r1, t, :], start=(t == 0), stop=(t == 4)),
                             reads=rd + [pt], writes=[po], sig=(t == 4))
                    for t, (kap, vap, r0, r1, on, rd) in enumerate(tiles):
                        S.op("pe", lambda e, on=on, t=t, r0=r0, r1=r1: e.matmul(pd[:, 0:64], lhsT=on[r0:r1, :], rhs=pt[r0:r1, t, :], start=(t == 0), stop=(t == 4)),
                             reads=[pt, CB, OB], writes=[pd], sig=(t == 4))
                    rdn = RD[n_it % 2]
                    S.op("dve", lambda e: e.reciprocal(out=rdn[:], in_=pd[:, 0:64]), reads=[pd], writes=[rdn])
                    S.op("dve", lambda e: e.tensor_tensor(out=ATT[:, hg, q0:q0 + 64], in0=po[:, 0:64], in1=rdn[:], op=ALU.mult), reads=[po, rdn], writes=[ATT])

                pend = None
                for n_it, (hh, c) in enumerate(order):
                    need_prep = (c >= 16 and hh == 0)
                    if need_prep:
                        if pend is not None:
                            stage_V(pend)
                            pend = None
                        prep_cache(c - 16)
                    ctx = stage_S(n_it, hh, c)
                    if pend is not None:
                        stage_V(pend)
                    pend = ctx
                stage_V(pend)
                pf_set[0] = [0, 1, 2, 3, 4, 5]
            S.barrier()

        if STOP == 3:
            S.barrier()
            return nc
        S.op("dve", lambda e: e.tensor_copy(out=XH3[:, :, :], in_=XTH[:, :, 509:512]), reads=[XTH], writes=[XH3])
        S.dma("sp", att_scr[:, :, :], ATT[:, :, :], reads=[ATT], writes=[ATTS])
        S.barrier()
        esh.close()
        esa.close()
        with ExitStack() as es3:
            NWG = [S.sb([64, 512], F32, es3, name=f"nwg{i}") for i in range(2)]
            SSDYG = [S.sb([128, 4, NOWN], BF16, es3, name=f"ssdyg{i}") for i in range(2)]
            DT = S.sb([64, 18, 64], F32, es3, name="dt")
            DA = S.sb([64, 18, 64], F32, es3, name="da")
            ECS = S.sb([64, 18, 64], F32, es3, name="ecs")
            WDT = S.sb([64, 18, 64], F32, es3, name="wdt")
            CDEC = S.sb([128, 18, 64], F32, es3, name="cdec")
            TMP = [S.sb([64, 64], F32, es3, name=f"tmpm{i}") for i in range(2)]
            wdt_w = load_w(w_in[:, C_DT:C_DT + 64], 16, 64)
            for c in range(18):
                t0 = c * 64
                ps = PF()
                for k in range(16):
                    S.op("pe", lambda e, k=k, ps=ps, t0=t0: e.matmul(ps[0:64, 0:64], lhsT=XTO[:, k, t0:t0 + 64], rhs=wdt_w[:, k, 0:64], start=(k == 0), stop=(k == 15)),
                         reads=[XTO, wdt_w], writes=[ps], sig=(k == 15))
                tm = TMP[c % 2]
                S.op("dve", lambda e, ps=ps, tm=tm: e.tensor_tensor(out=tm[:], in0=ps[0:64, 0:64], in1=HV[0:64, 0, :], op=ALU.add), reads=[ps, HV], writes=[tm])
                S.op("act", lambda e, tm=tm: e.activation(out=tm[:], in_=tm[:], func=AF.Exp), reads=[tm], writes=[tm])
                S.op("act", lambda e, tm=tm, c=c: e.activation(out=DT[:, c, :], in_=tm[:], func=AF.Ln, bias=1.0, scale=1.0), reads=[tm], writes=[DT])
                S.op("dve", lambda e, c=c: e.tensor_tensor(out=DA[:, c, :], in0=DT[:, c, :], in1=HV[0:64, 1, :], op=ALU.mult), reads=[DT, HV], writes=[DA])
                ps2 = PF()
                S.op("pe", lambda e, c=c, ps2=ps2: e.matmul(ps2[0:64, 0:64], lhsT=mle, rhs=DA[:, c, :], start=True, stop=True), reads=[DA, CF], writes=[ps2])
                S.op("pe", lambda e, c=c, ps2=ps2: e.matmul(ps2[0:64, 64:128], lhsT=lt, rhs=DA[:, c, :], start=True, stop=True), reads=[DA, CF], writes=[ps2])
                S.op("pe", lambda e, c=c, ps2=ps2: e.matmul(ps2[:, 128:192], lhsT=onesf, rhs=DA[:, c, :], start=True, stop=True), reads=[DA, CF], writes=[ps2])
                S.op("act", lambda e, c=c, ps2=ps2: e.activation(out=ECS[:, c, :], in_=ps2[0:64, 0:64], func=AF.Exp), reads=[ps2], writes=[ECS])
                S.op("act", lambda e, c=c, ps2=ps2: e.activation(out=WDT[:, c, :], in_=ps2[0:64, 64:128], func=AF.Exp), reads=[ps2], writes=[WDT])
                S.op("act", lambda e, c=c, ps2=ps2: e.activation(out=CDEC[:, c, :], in_=ps2[:, 128:192], func=AF.Exp), reads=[ps2], writes=[CDEC])
                S.op("dve", lambda e, c=c: e.tensor_tensor(out=WDT[:, c, :], in0=WDT[:, c, :], in1=DT[:, c, :], op=ALU.mult), reads=[WDT, DT], writes=[WDT])

            CIN = [S.sb([128, CINW], F32, es3, name=f"cin{i}") for i in range(2)]
            ACC = [S.sb([128, CINW - 3], F32, es3, name=f"acc{i}") for i in range(1)]
            XS = [S.sb([128, CINW - 3], BF16, es3, name=f"xs{i}") for i in range(6)]
            XD = [S.sb([64, (512 if DO_Y else 2)], BF16, es3, name=f"xd{i}") for i in range(2)]
            XW = [S.sb([64, 512], BF16, es3, name=f"xw{i}") for i in range(2)]
            XTS = [S.sb([64, 512], BF16, es3, name=f"xts{i}") for i in range(2)]
            DSK = S.sb([64, 8, 64], BF16, es3, name="dsk")
            BTK = [S.sb([64, 128], BF16, es3, name=f"btk{i}") for i in range(2)]
            SZ = [S.sb([64, 512], BF16, es3, name=f"sz{i}") for i in range(2)]
            CBM = [S.sb([64, (64 if DO_Y else 2)], F32, es3, name=f"cbm{i}") for i in range(2)]
            RR = [S.sb([64, (512 if DO_Y else 2)], F32, es3, name=f"rr{i}") for i in range(1)]
            EE = [S.sb([64, (512 if DO_Y else 2)], F32, es3, name=f"ee{i}") for i in range(1)]
            MT = [S.sb([64, (512 if DO_Y else 2)], BF16, es3, name=f"mt{i}") for i in range(2)]
            Y1 = [S.sb([64, (512 if DO_Y else 2)], F32, es3, name=f"y1{i}") for i in range(2)]
            Y2 = [S.sb([64, (512 if DO_Y else 2)], F32, es3, name=f"y2{i}") for i in range(1)]
            GO = [S.sb([64, (512 if DO_Y else 2)], BF16, es3, name=f"go{i}") for i in range(2)]
            SSQ = [S.sb([64, 2], F32, es3, name=f"ssq{i}") for i in range(2)]
            ZS = S.sb([128, 4, NOWN], BF16, es3, name="zs")
            HT = S.sb([128, 512], F32, es3, name="ht")
            HTB = S.sb([128, 512], BF16, es3, name="htb")
            HTM = S.sb([128, 512], F32, es3, name="htm")
            SIN = S.sb([128, 4, 128], F32, es3, name="sin")
            SOUT = [S.sb([128, 4, 128], F32, es3, name=f"sout{i}") for i in range(1)]
            soc = [0]

            def state_out(dst_rows):
                soc[0] += 1
                so = SOUT[0]
                ps = PF()
                for j in range(4):
                    S.op("pe", lambda e, ps=ps, j=j: e.transpose(ps[:, j * 128:(j + 1) * 128], HT[:, j * 128:(j + 1) * 128], identf),
                         reads=[HT, CF], writes=[ps], sig=(j == 3))
                S.op("act", lambda e, ps=ps, so=so: e.copy(out=so[:, :, :], in_=ps[:, 0:512].rearrange("p (j n) -> p j n", j=4)), reads=[ps], writes=[so])
                S.dma("sp", dst_rows.rearrange("(j p) n -> p j n", p=128), so[:, :, :], reads=[so])

            it = 0
            def load_group_w(g):
                wx_ = load_w(w_in[:, C_X + g * 512:C_X + (g + 1) * 512], 16, 512)
                wbc_ = load_w(w_in[:, C_B + g * 128:C_B + (g + 1) * 128], 16, 128)
                S.dma("pool", wbc_[:, 0:16, 128:256], w_in[:, C_C + g * 128:C_C + (g + 1) * 128].rearrange("(k p) c -> p k c", p=128), writes=[wbc_])
                wz_ = load_w(w_in[:, C_Z + g * 512:C_Z + (g + 1) * 512], 16, 512)
                return wz_, wx_, wbc_

            gw = load_group_w(0)
            for g in range(8):
                wz, wx, wbc = gw
                nwg = NWG[g % 2]
                ssdyg = SSDYG[g % 2]
                S.dma("sp", nwg[:], norm_w[g * 512:(g + 1) * 512].partition_broadcast(64), writes=[nwg])
                S.op("dve", lambda e, g=g: e.tensor_tensor(out=DSK[:, :, :], in0=identb[0:64, 0:64].rearrange("p (o q) -> p o q", o=1).to_broadcast([64, 8, 64]),
                                                         in1=HV[0:64, 2, g * 8:(g + 1) * 8].rearrange("p (h o) -> p h o", o=1).to_broadcast([64, 8, 64]), op=ALU.mult),
                     reads=[CB, HV], writes=[DSK])
                for fi in range(6):
                    if fi < 4:
                        wsl, wc0, ftg = wx, fi * 128, g * 4 + fi
                    elif fi == 4:
                        wsl, wc0, ftg = wbc, 0, 32 + g
                    else:
                        wsl, wc0, ftg = wbc, 128, 40 + g
                    cin = CIN[fi % 2]
                    segs = [(XH3, 0, 3, 0), (XTO, 0, 512, 3), (XTO, 512, 512, 515), (XTO, 1024, 64, 1030), (XTO, 1088, 64, 1097)]
                    for (src, c0, n, d0) in segs:
                        ps = PF()
                        for k in range(16):
                            S.op("pe", lambda e, k=k, ps=ps, src=src, c0=c0, n=n, wsl=wsl, wc0=wc0: e.matmul(ps[:, 0:n], lhsT=wsl[:, k, wc0:wc0 + 128],
                                                                                                         rhs=src[:, k, c0:c0 + n], start=(k == 0), stop=(k == 15)),
                                 reads=[wsl, src], writes=[ps], sig=(k == 15))
                        S.op("act", lambda e, ps=ps, cin=cin, n=n, d0=d0: e.copy(out=cin[:, d0:d0 + n], in_=ps[:, 0:n]), reads=[ps], writes=[cin])
                    for s in range(2):
                        S.op("dve", lambda e, cin=cin, s=s, ftg=ftg: e.tensor_copy(out=cin[:, 1027 + 67 * s:1030 + 67 * s], in_=SH[:, ftg, s, :]),
                             reads=[SH], writes=[cin])
                    with nc.allow_non_contiguous_dma(reason="conv state rows"):
                        S.dma("sp", convp_o[:, ftg * 128:(ftg + 1) * 128].rearrange("r f -> f r"), cin[:, 1024:1027], reads=[cin])
                        for s in range(2):
                            S.dma("sp", convs_o[s, :, ftg * 128:(ftg + 1) * 128].rearrange("r f -> f r"), cin[:, 1091 + 67 * s:1094 + 67 * s], reads=[cin])
                    acc = ACC[0]
                    W_ = CINW - 3
                    S.op("dve", lambda e, cin=cin, acc=acc, ftg=ftg: e.tensor_scalar(out=acc[:], in0=cin[:, 0:W_], scalar1=CW[:, ftg, 0:1], scalar2=CBI[:, ftg:ftg + 1],
                                                                                  op0=ALU.mult, op1=ALU.add), reads=[cin, CW, CBI], writes=[acc])
                    for tap in range(1, 4):
                        S.op("dve", lambda e, cin=cin, acc=acc, ftg=ftg, tap=tap: e.scalar_tensor_tensor(
                            out=acc[:], in0=cin[:, tap:tap + W_], scalar=CW[:, ftg, tap:tap + 1], in1=acc[:], op0=ALU.mult, op1=ALU.add),
                            reads=[cin, CW, acc], writes=[acc])
                    S.op("act", lambda e, acc=acc, fi=fi: e.activation(out=XS[fi][:], in_=acc[:], func=AF.Silu), reads=[acc], writes=[XS[fi]])
                for ct in range(4):
                    for tg in range(3):
                        ps = PF()
                        for k in range(16):
                            S.op("pe", lambda e, k=k, ps=ps, ct=ct, tg=tg: e.matmul(ps[:, 0:384], lhsT=wz[:, k, ct * 128:(ct + 1) * 128], rhs=XTO[:, k, tg * 384:(tg + 1) * 384],
                                                                                start=(k == 0), stop=(k == 15)), reads=[wz, XTO], writes=[ps], sig=(k == 15))
                        S.op("act", lambda e, ps=ps, ct=ct, tg=tg: e.activation(out=ZS[:, ct, tg * 384:(tg + 1) * 384], in_=ps[:, 0:384], func=AF.Silu), reads=[ps], writes=[ZS])
                if g + 1 < 8:
                    gw = load_group_w(g + 1)

                def stage_T(c):
                    pf_set[0] = [0, 1]
                    pb_set[0] = [5, 6]
                    i2 = c % 2
                    if c < 16:
                        q0, t0 = c * 64, c * 64
                    else:
                        s = c - 16
                        q0, t0 = 1027 + 67 * s, NPR + 64 * s

                    def bch(src, c=c, g=g, np_=64):
                        return src[0:np_, c, g * 8:(g + 1) * 8].rearrange("p (h o) -> p h o", o=1).to_broadcast([np_, 8, 64])
                    xd, xw, xts, btk = XD[i2], XW[i2], XTS[i2], BTK[i2]
                    sz = SZ[i2]
                    mt = MT[i2]
                    if DO_Y:
                        rr = RR[0]
                        S.op("dve", lambda e, rr=rr: e.tensor_tensor(out=rr[:, :].rearrange("p (h q) -> p h q", h=8), in0=bch(DA),
                                                                     in1=mle.rearrange("p (o q) -> p o q", o=1).to_broadcast([64, 8, 64]), op=ALU.mult),
                             reads=[DA, CF], writes=[rr])
                    pb = PB()
                    for j in range(4):
                        S.op("pe", lambda e, pb=pb, j=j, q0=q0: e.transpose(pb[0:64, j * 128:(j + 1) * 128], XS[j][:, q0:q0 + 64], identb), reads=[XS[j]], writes=[pb], sig=False)
                    S.op("pe", lambda e, pb=pb, q0=q0: e.transpose(pb[0:64, 512:640], XS[4][:, q0:q0 + 64], identb), reads=[XS[4]], writes=[pb])
                    x3 = pb[0:64, 0:512].rearrange("p (h q) -> p h q", h=8)


                    if DO_Y:
                        S.op("dve", lambda e, xd=xd, x3=x3: e.tensor_tensor(out=xd[:, :].rearrange("p (h q) -> p h q", h=8), in0=x3, in1=bch(DT), op=ALU.mult),
                             reads=[pb, DT], writes=[xd])
                    if DO_Y:
                        S.op("act", lambda e, xts=xts, pb=pb: e.copy(out=xts[:], in_=pb[0:64, 0:512]), reads=[pb], writes=[xts])
                    S.op("pool", lambda e, xw=xw, xts=xts: e.tensor_tensor(out=xw[:, :].rearrange("p (h q) -> p h q", h=8), in0=xts[:, :].rearrange("p (h q) -> p h q", h=8),
                                                                    in1=bch(WDT), op=ALU.mult), reads=[xts, WDT], writes=[xw])
                    S.op("act", lambda e, pb=pb, btk=btk: e.copy(out=btk[:], in_=pb[0:64, 512:640]), reads=[pb], writes=[btk])
                    if DO_Y:
                        zb = PB()
                        for j in range(4):
                            S.op("pe", lambda e, zb=zb, j=j, t0=t0: e.transpose(zb[0:64, j * 128:(j + 1) * 128], ZS[:, j, t0:t0 + 64], identb), reads=[ZS], writes=[zb], sig=(j == 3))
                        S.op("act", lambda e, zb=zb, sz=sz: e.copy(out=sz[:], in_=zb[0:64, 0:512]), reads=[zb], writes=[sz])
                        pc = PF()
                        S.op("pe", lambda e, pc=pc, q0=q0: e.matmul(pc[0:64, 0:64], lhsT=XS[4][:, q0:q0 + 64], rhs=XS[5][:, q0:q0 + 64], start=True, stop=True),
                             reads=[XS[4], XS[5]], writes=[pc])
                        cbm = CBM[i2]
                        S.op("dve", lambda e, pc=pc, cbm=cbm: e.tensor_tensor(out=cbm[:], in0=pc[0:64, 0:64], in1=mle, op=ALU.mult), reads=[pc, CF], writes=[cbm])
                        rr, ee = RR[0], EE[0]
                        pd = PF()
                        S.op("pe", lambda e, pd=pd, rr=rr: e.matmul(pd[0:64, 0:512], lhsT=lt, rhs=rr[:], start=True, stop=True), reads=[rr, CF], writes=[pd])
                        S.op("act", lambda e, pd=pd, ee=ee: e.activation(out=ee[:], in_=pd[0:64, 0:512], func=AF.Exp), reads=[pd], writes=[ee])
                        S.op("pool", lambda e, ee=ee, mt=mt, cbm=cbm: e.tensor_tensor(
                            out=mt[:, :].rearrange("p (h q) -> p h q", h=8), in0=ee[:, :].rearrange("p (h q) -> p h q", h=8),
                            in1=cbm[:, :].rearrange("p (o q) -> p o q", o=1).to_broadcast([64, 8, 64]), op=ALU.mult), reads=[ee, cbm], writes=[mt])

                def stage_Y(c):
                    pf_set[0] = [2, 3, 4]
                    pb_set[0] = [7]
                    i2 = c % 2
                    if c < 16:
                        q0, t0 = c * 64, c * 64
                    else:
                        s = c - 16
                        q0, t0 = 1027 + 67 * s, NPR + 64 * s

                    def bch(src, c=c, g=g, np_=64):
                        return src[0:np_, c, g * 8:(g + 1) * 8].rearrange("p (h o) -> p h o", o=1).to_broadcast([np_, 8, 64])
                    xd, xw, xts, btk = XD[i2], XW[i2], XTS[i2], BTK[i2]
                    sz = SZ[i2]
                    mt = MT[i2]
                    if c == 0:
                        if DO_PRE:
                            S.dma("sp", HT[:], hcs[:, g * 512:(g + 1) * 512], reads=[HCS], writes=[HT])
                        else:
                            S.op("dve", lambda e: e.memset(HT[:], 0.0), writes=[HT])
                        S.op("act", lambda e: e.copy(out=HTB[:], in_=HT[:]), reads=[HT], writes=[HTB])
                    elif c >= 16:
                        s = c - 16
                        S.dma("sp", SIN[:, :, :], sssm[s, g * 512:(g + 1) * 512, :].rearrange("(j p) n -> p j n", p=128), writes=[SIN])
                        ps = PF()
                        for j in range(4):
                            S.op("pe", lambda e, ps=ps, j=j: e.transpose(ps[:, j * 128:(j + 1) * 128], SIN[:, j, :], identf), reads=[SIN, CF], writes=[ps], sig=(j == 3))
                        S.op("act", lambda e, ps=ps: e.copy(out=HT[:], in_=ps[:, 0:512]), reads=[ps], writes=[HT])
                        S.op("act", lambda e: e.copy(out=HTB[:], in_=HT[:]), reads=[HT], writes=[HTB])
                    if DO_Y:
                        po = PF()
                        S.op("pe", lambda e, po=po, q0=q0: e.matmul(po[0:64, 0:512], lhsT=XS[5][:, q0:q0 + 64], rhs=HTB[:], start=True, stop=True),
                             reads=[XS[5], HTB], writes=[po])
                        y1 = Y1[i2]
                        S.op("dve", lambda e, po=po, y1=y1: e.tensor_tensor(out=y1[:, :].rearrange("p (h q) -> p h q", h=8), in0=po[0:64, 0:512].rearrange("p (h q) -> p h q", h=8),
                                                                           in1=bch(ECS), op=ALU.mult), reads=[po, ECS], writes=[y1])
                    pst = PF()
                    S.op("pe", lambda e, pst=pst, btk=btk, xw=xw: e.matmul(pst[:, 0:512], lhsT=btk[:], rhs=xw[:], start=True, stop=True), reads=[btk, xw], writes=[pst])
                    S.op("dve", lambda e: e.tensor_tensor(out=HTM[:, :].rearrange("p (h q) -> p h q", h=8), in0=HT[:, :].rearrange("p (h q) -> p h q", h=8),
                                                          in1=bch(CDEC, np_=128), op=ALU.mult), reads=[HT, CDEC], writes=[HTM])
                    S.op("dve", lambda e, pst=pst: e.tensor_tensor(out=HT[:], in0=HTM[:], in1=pst[:, 0:512], op=ALU.add), reads=[HTM, pst], writes=[HT])
                    S.op("act", lambda e: e.copy(out=HTB[:], in_=HT[:]), reads=[HT], writes=[HTB])
                    if c == 15:
                        state_out(ssmp_o[g * 512:(g + 1) * 512, :])
                    elif c >= 16:
                        state_out(ssms_o[c - 16, g * 512:(g + 1) * 512, :])

                    if DO_Y:
                        py = PF()
                        for h in range(8):
                            S.op("pe", lambda e, py=py, h=h, mt=mt, xd=xd: e.matmul(py[0:64, h * 64:(h + 1) * 64], lhsT=mt[:, h * 64:(h + 1) * 64], rhs=xd[:, h * 64:(h + 1) * 64],
                                                                                  start=True, stop=False), reads=[mt, xd], writes=[py], sig=False)
                            S.op("pe", lambda e, py=py, h=h, xts=xts: e.matmul(py[0:64, h * 64:(h + 1) * 64], lhsT=DSK[:, h, :], rhs=xts[:, h * 64:(h + 1) * 64],
                                                                             start=False, stop=True), reads=[DSK, xts], writes=[py], sig=(h == 7))
                        y1, y2 = Y1[i2], Y2[0]
                        S.op("dve", lambda e, py=py, y1=y1, y2=y2: e.tensor_tensor(out=y2[:], in0=py[0:64, 0:512], in1=y1[:], op=ALU.add), reads=[py, y1], writes=[y2])
                        S.op("pool", lambda e, y2=y2, sz=sz, y1=y1: e.tensor_tensor(out=y1[:], in0=y2[:], in1=sz[:], op=ALU.mult), reads=[y2, sz], writes=[y1])
                        ssq = SSQ[i2]
                        S.op("act", lambda e, y1=y1, y2=y2, ssq=ssq: e.activation(out=y2[:], in_=y1[:], func=AF.Square, accum_out=ssq[:, 0:1]), reads=[y1], writes=[y2, ssq])
                        S.op("act", lambda e, ssq=ssq: e.activation(out=ssq[:, 1:2], in_=ssq[:, 0:1], func=AF.Ln, bias=1e-5, scale=1.0 / 512.0), reads=[ssq], writes=[ssq])
                        S.op("act", lambda e, ssq=ssq: e.activation(out=ssq[:, 1:2], in_=ssq[:, 1:2], func=AF.Exp, scale=-0.5), reads=[ssq], writes=[ssq])

                def stage_Y2(c):
                    pf_set[0] = [2, 3, 4]
                    pb_set[0] = [7]
                    i2 = c % 2
                    if c < 16:
                        q0, t0 = c * 64, c * 64
                    else:
                        s = c - 16
                        q0, t0 = 1027 + 67 * s, NPR + 64 * s

                    def bch(src, c=c, g=g, np_=64):
                        return src[0:np_, c, g * 8:(g + 1) * 8].rearrange("p (h o) -> p h o", o=1).to_broadcast([np_, 8, 64])
                    xd, xw, xts, btk = XD[i2], XW[i2], XTS[i2], BTK[i2]
                    sz = SZ[i2]
                    mt = MT[i2]
                    y1 = Y1[i2]
                    ssq = SSQ[i2]
                    if DO_Y:
                        go = GO[i2]
                        S.op("dve", lambda e, y1=y1, ssq=ssq, go=go, nwg=nwg: e.scalar_tensor_tensor(out=go[:], in0=y1[:], scalar=ssq[:, 1:2], in1=nwg[:],
                                                                                               op0=ALU.mult, op1=ALU.mult), reads=[y1, ssq, nwg], writes=[go])
                        pb2 = PB()
                        for j in range(4):
                            S.op("pe", lambda e, pb2=pb2, j=j, go=go: e.transpose(pb2[:, j * 64:(j + 1) * 64], go[:, j * 128:(j + 1) * 128], identb[0:64, 0:64]),
                                 reads=[go], writes=[pb2], sig=(j == 3))
                        S.op("act", lambda e, pb2=pb2, ssdyg=ssdyg, t0=t0: e.copy(out=ssdyg[:, :, t0:t0 + 64], in_=pb2[:, 0:256].rearrange("p (j t) -> p j t", j=4)),
                             reads=[pb2], writes=[ssdyg])

                stage_T(0)
                for c in range(18):
                    if c + 1 < 18:
                        stage_T(c + 1)
                    stage_Y(c)
                    if c >= 1:
                        stage_Y2(c - 1)
                stage_Y2(17)
                pf_set[0] = [0, 1, 2, 3, 4, 5]
                pb_set[0] = [6, 7]
                if DO_Y:
                    S.dma("sp", ssdy_scr[g], ssdyg[:, :, :], reads=[ssdyg], writes=[SSDYS])
            S.barrier()

        if STOP == 4:
            return nc
        FMAX = int(nc.vector.BN_STATS_FMAX)
        SDIM = int(nc.vector.BN_STATS_DIM)
        nst = (D + FMAX - 1) // FMAX
        assert D % nst == 0
        fch = D // nst

        def layer_norm_rows(src_ap, dst_ap, G, Bb, ST, MV, rd_tiles, wr_tiles):
            for q in range(nst):
                S.op("dve", lambda e, q=q: e.bn_stats(out=ST[:, q, :], in_=src_ap[:, q * fch:(q + 1) * fch]), reads=rd_tiles, writes=[ST])
            S.op("dve", lambda e: e.bn_aggr(out=MV[:, 0:2], in_=ST[:, :, :]), reads=[ST], writes=[MV])
            S.op("act", lambda e: e.activation(out=MV[:, 2:3], in_=MV[:, 1:2], func=AF.Ln, bias=1e-5, scale=1.0), reads=[MV], writes=[MV])
            S.op("act", lambda e: e.activation(out=MV[:, 2:3], in_=MV[:, 2:3], func=AF.Exp, scale=-0.5), reads=[MV], writes=[MV])
            S.op("dve", lambda e: e.tensor_scalar(out=dst_ap, in0=src_ap, scalar1=MV[:, 0:1], scalar2=MV[:, 2:3], op0=ALU.subtract, op1=ALU.mult),
                 reads=rd_tiles + [MV], writes=wr_tiles)
            S.op("dve", lambda e: e.tensor_tensor(out=dst_ap, in0=dst_ap, in1=G[:], op=ALU.mult), reads=wr_tiles + [G], writes=wr_tiles)
            S.op("dve", lambda e: e.tensor_tensor(out=dst_ap, in0=dst_ap, in1=Bb[:], op=ALU.add), reads=wr_tiles + [Bb], writes=wr_tiles)

        NTB = 384
        hscr = nc.dram_tensor("hscr", [NOWN, D], F32).ap()
        yscr = nc.dram_tensor("yscr", [NOWN, D], F32).ap()
        gate_scr = nc.dram_tensor("gate_scr", [8, 128, 4, NOWN], BF16).ap()
        HSCR = T(None)
        YSCR = T(None)
        GSCR = T(None)
        ht_scr = nc.dram_tensor("ht_scr", [128, 16, NOWN], BF16).ap()
        HTSCR = T(None)
        with ExitStack() as esg:
            XTa = XTO
            SGB = [S.sb([128, 4, NOWN], BF16, esg, name=f"sgb{i}") for i in range(2)]
            for wi in range(8):
                col = (C_GS if wi < 4 else C_GA) + (wi % 4) * 512
                wsl = load_w(w_in[:, col:col + 512], 16, 512)
                sgb = SGB[wi % 2]
                for ct in range(4):
                    for tg in range(3):
                        ps = PF()
                        for k in range(16):
                            S.op("pe", lambda e, k=k, ps=ps, ct=ct, tg=tg: e.matmul(ps[:, 0:NTB], lhsT=wsl[:, k, ct * 128:(ct + 1) * 128], rhs=XTa[:, k, tg * NTB:(tg + 1) * NTB],
                                                                                start=(k == 0), stop=(k == 15)), reads=[wsl, XTa], writes=[ps], sig=(k == 15))
                        S.op("act", lambda e, ps=ps, ct=ct, tg=tg: e.activation(out=sgb[:, ct, tg * NTB:(tg + 1) * NTB], in_=ps[:, 0:NTB], func=AF.Sigmoid), reads=[ps], writes=[sgb])
                S.dma("sp", gate_scr[wi], sgb[:, :, :], reads=[sgb], writes=[GSCR])
            S.barrier()
        esxo.close()
        esmix = ExitStack()
        MIX = S.sb([128, 16, NOWN], BF16, esmix, name="mixall")
        with ExitStack() as ess:
            SYa = S.sb([128, 32, NOWN], BF16, ess, name="sya")
            GB = [S.sb([128, 4, NOWN], BF16, ess, name=f"gb{i}") for i in range(2)]
            for g in range(8):
                S.dma("sp", SYa[:, g * 4:(g + 1) * 4, :], ssdy_scr[g], reads=[SSDYS], writes=[SYa])
            for cb in range(4):
                wso0 = load_w(w_so[0:2048, cb * 512:(cb + 1) * 512], 16, 512)
                wso1 = load_w(w_so[2048:4096, cb * 512:(cb + 1) * 512], 16, 512)
                gb = GB[cb % 2]
                S.dma("sp", gb[:, :, :], gate_scr[cb], reads=[GSCR], writes=[gb])
                for ct in range(4):
                    for tg in range(3):
                        ps = PF()
                        for k in range(32):
                            wsl = wso0 if k < 16 else wso1
                            S.op("pe", lambda e, k=k, ps=ps, wsl=wsl, ct=ct, tg=tg: e.matmul(ps[:, 0:NTB], lhsT=wsl[:, k % 16, ct * 128:(ct + 1) * 128], rhs=SYa[:, k, tg * NTB:(tg + 1) * NTB],
                                                                                         start=(k == 0), stop=(k == 31)), reads=[wsl, SYa], writes=[ps], sig=(k == 31))
                        S.op("dve", lambda e, ps=ps, ct=ct, tg=tg, cb=cb, gb=gb: e.tensor_tensor(out=MIX[:, cb * 4 + ct, tg * NTB:(tg + 1) * NTB], in0=ps[:, 0:NTB],
                                                                                                in1=gb[:, ct, tg * NTB:(tg + 1) * NTB], op=ALU.mult), reads=[ps, gb], writes=[MIX])
            S.barrier()
        with ExitStack() as esa5:
            ATa = S.sb([128, 16, NOWN], BF16, esa5, name="ata")
            GB = [S.sb([128, 4, NOWN], BF16, esa5, name=f"gba{i}") for i in range(2)]
            TCa = [S.sb([128, NTB], F32, esa5, name=f"tca{i}") for i in range(2)]
            S.dma("sp", ATa[:, :, :], att_scr[:, :, :], reads=[ATTS], writes=[ATa])
            ita = 0
            for cb in range(4):
                wao = load_w(w_ao[:, cb * 512:(cb + 1) * 512], 16, 512)
                gb = GB[cb % 2]
                S.dma("sp", gb[:, :, :], gate_scr[4 + cb], reads=[GSCR], writes=[gb])
                for ct in range(4):
                    for tg in range(3):
                        ita += 1
                        ps = PF()
                        for k in range(16):
                            S.op("pe", lambda e, k=k, ps=ps, ct=ct, tg=tg: e.matmul(ps[:, 0:NTB], lhsT=wao[:, k, ct * 128:(ct + 1) * 128], rhs=ATa[:, k, tg * NTB:(tg + 1) * NTB],
                                                                                start=(k == 0), stop=(k == 15)), reads=[wao, ATa], writes=[ps], sig=(k == 15))
                        tc_ = TCa[ita % 2]
                        S.op("dve", lambda e, ps=ps, ct=ct, tg=tg, gb=gb, tc_=tc_: e.tensor_tensor(out=tc_[:], in0=ps[:, 0:NTB], in1=gb[:, ct, tg * NTB:(tg + 1) * NTB], op=ALU.mult),
                             reads=[ps, gb], writes=[tc_])
                        S.op("dve", lambda e, ct=ct, tg=tg, cb=cb, tc_=tc_: e.tensor_tensor(out=MIX[:, cb * 4 + ct, tg * NTB:(tg + 1) * NTB], in0=tc_[:],
                                                                                           in1=MIX[:, cb * 4 + ct, tg * NTB:(tg + 1) * NTB], op=ALU.add), reads=[tc_, MIX], writes=[MIX])
            S.barrier()
        with ExitStack() as esb5:
            YS4 = [S.sb([128, 512], F32, esb5, name=f"ys4{i}") for i in range(2)]
            for cb in range(4):
                wo = load_w(w_o[:, cb * 512:(cb + 1) * 512], 16, 512)
                for tt in range(9):
                    ps = PF()
                    for k in range(16):
                        S.op("pe", lambda e, k=k, ps=ps, tt=tt: e.matmul(ps[:, 0:512], lhsT=MIX[:, k, tt * 128:(tt + 1) * 128], rhs=wo[:, k, 0:512],
                                                                     start=(k == 0), stop=(k == 15)), reads=[MIX, wo], writes=[ps], sig=(k == 15))
                    ys = YS4[tt % 2]
                    S.op("act", lambda e, ps=ps, ys=ys: e.copy(out=ys[:], in_=ps[:, 0:512]), reads=[ps], writes=[ys])
                    S.dma("sp", yscr[tt * 128:(tt + 1) * 128, cb * 512:(cb + 1) * 512], ys[:], reads=[ys], writes=[YSCR])
            LG = S.sb([128, D], F32, esb5, name="lng")
            LB = S.sb([128, D], F32, esb5, name="lnb")
            ST = S.sb([128, nst, SDIM], F32, esb5, name="bnst")
            MV = S.sb([128, 4], F32, esb5, name="bnmv")
            S.dma("sp", LG[:], ln1_g.partition_broadcast(128), writes=[LG])
            S.dma("sp", LB[:], ln1_b.partition_broadcast(128), writes=[LB])
            XR = [S.sb([128, D], F32, esb5, name=f"xr{i}") for i in range(2)]
            AR4 = [S.sb([128, D], F32, esb5, name=f"ar4{i}") for i in range(2)]
            HR4 = [S.sb([128, D], F32, esb5, name=f"hr4{i}") for i in range(2)]
            HB = [S.sb([128, D], BF16, esb5, name=f"hb{i}") for i in range(2)]
            HTS = [S.sb([128, 16, 128], BF16, esb5, name=f"hts{i}") for i in range(2)]
            def ld1(tt):
                S.dma("sp", XR[tt % 2][:], xo[tt * 128:(tt + 1) * 128, :], writes=[XR[tt % 2]])
                S.dma("sp", AR4[tt % 2][:], yscr[tt * 128:(tt + 1) * 128, :], reads=[YSCR], writes=[AR4[tt % 2]])

            ld1(0)
            for tt in range(9):
                xr, ar, hr, hb = XR[tt % 2], AR4[tt % 2], HR4[tt % 2], HB[tt % 2]
                if tt + 1 < 9:
                    ld1(tt + 1)
                S.op("dve", lambda e, xr=xr, ar=ar: e.scalar_tensor_tensor(out=ar[:], in0=xr[:], scalar=float(ALPHA), in1=ar[:], op0=ALU.mult, op1=ALU.add),
                     reads=[xr, ar], writes=[ar])
                layer_norm_rows(ar[:], hr[:], LG, LB, ST, MV, [ar], [hr])
                S.op("act", lambda e, hb=hb, hr=hr: e.copy(out=hb[:], in_=hr[:]), reads=[hr], writes=[hb])
                S.dma("sp", hscr[tt * 128:(tt + 1) * 128, :], hr[:], reads=[hr], writes=[HSCR])
                for half in range(2):
                    pb = PB()
                    for j in range(8):
                        k = half * 8 + j
                        S.op("pe", lambda e, k=k, j=j, pb=pb, hb=hb: e.transpose(pb[:, j * 128:(j + 1) * 128], hb[:, k * 128:(k + 1) * 128], identb),
                             reads=[hb], writes=[pb], sig=(j == 7))
                    hts = HTS[tt % 2]
                    if half == 0:
                        S.op("act", lambda e, pb=pb, hts=hts: e.copy(out=hts[:, 0:8, :], in_=pb[:, :].rearrange("p (k t) -> p k t", k=8)), reads=[pb], writes=[hts])
                    else:
                        S.op("dve", lambda e, pb=pb, hts=hts: e.tensor_copy(out=hts[:, 8:16, :], in_=pb[:, :].rearrange("p (k t) -> p k t", k=8)), reads=[pb], writes=[hts])
                S.dma("sp", ht_scr[:, :, tt * 128:(tt + 1) * 128], HTS[tt % 2][:, :, :], reads=[HTS[tt % 2]], writes=[HTSCR])
            S.barrier()
        esmix.close()
        HT_ = S.sb([128, 16, NOWN], BF16, name="htall")
        S.dma("sp", HT_[:, :, :], ht_scr[:, :, :], reads=[HTSCR], writes=[HT_])
        with ExitStack() as es5:
            ACT_ = S.sb([128, 44, NOWN], BF16, es5, name="actt")
            SGT = [S.sb([128, NTB], F32, es5, name=f"sgt{i}") for i in range(2)]
            YS = [S.sb([128, 512], F32, es5, name=f"ys{i}") for i in range(2)]
            it5 = 0
            for fb in range(11):
                wg = load_w(w_gu[:, fb * 512:(fb + 1) * 512], 16, 512)
                wu = load_w(w_gu[:, DFF + fb * 512:DFF + (fb + 1) * 512], 16, 512)
                for ct in range(4):
                    for tg in range(3):
                        it5 += 1
                        psg = PF()
                        for k in range(16):
                            S.op("pe", lambda e, k=k, psg=psg, ct=ct, tg=tg: e.matmul(psg[:, 0:NTB], lhsT=wg[:, k, ct * 128:(ct + 1) * 128], rhs=HT_[:, k, tg * NTB:(tg + 1) * NTB],
                                                                                  start=(k == 0), stop=(k == 15)), reads=[wg, HT_], writes=[psg], sig=(k == 15))
                        psu = PF()
                        for k in range(16):
                            S.op("pe", lambda e, k=k, psu=psu, ct=ct, tg=tg: e.matmul(psu[:, 0:NTB], lhsT=wu[:, k, ct * 128:(ct + 1) * 128], rhs=HT_[:, k, tg * NTB:(tg + 1) * NTB],
                                                                                  start=(k == 0), stop=(k == 15)), reads=[wu, HT_], writes=[psu], sig=(k == 15))
                        sgt = SGT[it5 % 2]
                        S.op("act", lambda e, psg=psg, sgt=sgt: e.activation(out=sgt[:], in_=psg[:, 0:NTB], func=AF.Silu), reads=[psg], writes=[sgt])
                        S.op("dve", lambda e, psu=psu, sgt=sgt, fb=fb, ct=ct, tg=tg: e.tensor_tensor(out=ACT_[:, fb * 4 + ct, tg * NTB:(tg + 1) * NTB], in0=psu[:, 0:NTB], in1=sgt[:], op=ALU.mult),
                             reads=[psu, sgt], writes=[ACT_])
            for cb in range(4):
                wd = [load_w(w_dn[0:2048, cb * 512:(cb + 1) * 512], 16, 512), load_w(w_dn[2048:4096, cb * 512:(cb + 1) * 512], 16, 512),
                      load_w(w_dn[4096:DFF, cb * 512:(cb + 1) * 512], 12, 512)]
                for tt in range(9):
                    ps = PF()
                    for k in range(44):
                        wsl = wd[k // 16]
                        S.op("pe", lambda e, k=k, ps=ps, tt=tt, wsl=wsl: e.matmul(ps[:, 0:512], lhsT=ACT_[:, k, tt * 128:(tt + 1) * 128], rhs=wsl[:, k % 16, 0:512],
                                                                              start=(k == 0), stop=(k == 43)), reads=[ACT_, wsl], writes=[ps], sig=(k == 43))
                    ys = YS[tt % 2]
                    S.op("act", lambda e, ps=ps, ys=ys: e.copy(out=ys[:], in_=ps[:, 0:512]), reads=[ps], writes=[ys])
                    S.dma("sp", yscr[tt * 128:(tt + 1) * 128, cb * 512:(cb + 1) * 512], ys[:], reads=[ys], writes=[YSCR])
            S.barrier()
        with ExitStack() as es6:
            LG = S.sb([128, D], F32, es6, name="lng2")
            LB = S.sb([128, D], F32, es6, name="lnb2")
            ST = S.sb([128, nst, SDIM], F32, es6, name="bnst2")
            MV = S.sb([128, 4], F32, es6, name="bnmv2")
            S.dma("sp", LG[:], ln2_g.partition_broadcast(128), writes=[LG])
            S.dma("sp", LB[:], ln2_b.partition_broadcast(128), writes=[LB])
            AR = [S.sb([128, D], F32, es6, name=f"ar{i}") for i in range(2)]
            HR = [S.sb([128, D], F32, es6, name=f"hr{i}") for i in range(2)]
            YT = [S.sb([128, D], F32, es6, name=f"yt{i}") for i in range(2)]
            def ld2(tt):
                S.dma("sp", AR[tt % 2][:], yscr[tt * 128:(tt + 1) * 128, :], reads=[YSCR], writes=[AR[tt % 2]])
                S.dma("sp", HR[tt % 2][:], hscr[tt * 128:(tt + 1) * 128, :], reads=[HSCR], writes=[HR[tt % 2]])

            ld2(0)
            for tt in range(9):
                ar, hr, yt = AR[tt % 2], HR[tt % 2], YT[tt % 2]
                if tt + 1 < 9:
                    ld2(tt + 1)
                S.op("dve", lambda e, ar=ar, hr=hr: e.scalar_tensor_tensor(out=ar[:], in0=hr[:], scalar=float(ALPHA), in1=ar[:], op0=ALU.mult, op1=ALU.add),
                     reads=[hr, ar], writes=[ar])
                layer_norm_rows(ar[:], yt[:], LG, LB, ST, MV, [ar], [yt])
                S.dma("sp", y_o[tt * 128:(tt + 1) * 128, :], yt[:], reads=[yt])
            S.barrier()
        S.barrier()
    return nc


_CACHE = {}


def _host_consts():
    cst = np.zeros((128, 512), np.float32)
    cst[:, 0:128] = np.eye(128, dtype=np.float32)
    k = np.arange(64)
    cst[0:64, 128:192] = (k[:, None] <= k[None, :]).astype(np.float32)
    cst[0:64, 192:256] = (k[:, None] > k[None, :]).astype(np.float32)
    cst[0:64, 256:384] = 1.0
    return cst


def _bias_tables(rel_bias):
    rb = np.asarray(rel_bias, np.float32)
    p = np.arange(128)[:, None, None]
    tt = np.arange(2)[None, :, None]
    i = np.arange(64)[None, None, :]
    out = np.zeros((3, 128, 16, 2, 64), np.float32)
    for v in range(3):
        if v == 0:
            jb = 128 * (3 + tt) + p + 0 * i
        elif v == 1:
            jb = 128 * (3 + tt) + p - 64 + 0 * i
        else:
            jb = np.where(tt == 0, 384 + p, 512 + (p - 64)) + 0 * i
            jb = np.where((tt == 1) & (p < 64), 10000, jb)
        idx = np.minimum(i - jb + 640, 256)
        idx = np.where((jb < 0) | (jb >= 576), 256, idx)
        idx = np.clip(idx, 0, 256)
        out[v] = np.transpose(rb[:, idx], (1, 0, 2, 3))
    return out.reshape(3, 128, 16 * 128)


def kernel(x_prompt, x_sample, cache_k, cache_v, state_conv, state_ssm, w_in, conv_w, conv_b,
           dt_bias, a_log, d_skip, ssd_norm_w, rel_bias, w_ssd_out, w_att_out, w_o, ln1_g, ln1_b,
           w_gate_up, w_down, ln2_g, ln2_b):
    f = lambda a: np.ascontiguousarray(np.asarray(a, dtype=np.float32))
    x_prompt = f(x_prompt); x_sample = f(x_sample)
    cache_k = f(cache_k); cache_v = f(cache_v); state_conv = f(state_conv); state_ssm = f(state_ssm)
    if "nc" not in _CACHE:
        _CACHE["nc"] = build_program()
    nc = _CACHE["nc"]
    shared = dict(
        xp=f(x_prompt[0, :NPRE * NPR]), w_in=f(w_in[0]), conv_w=f(conv_w[0]), conv_b=f(conv_b[0]),
        dt_bias=f(dt_bias[0]), a_log=f(a_log[0]), d_skip=f(d_skip[0]), norm_w=f(ssd_norm_w[0]),
        bt=_bias_tables(np.asarray(rel_bias)[0]), relc=f(np.asarray(rel_bias)[0][:, 256]),
        w_so=f(w_ssd_out[0]), w_ao=f(w_att_out[0]), w_o=f(w_o[0]), ln1_g=f(ln1_g[0]), ln1_b=f(ln1_b[0]),
        ln2_g=f(ln2_g[0]), ln2_b=f(ln2_b[0]), w_gu=f(w_gate_up[0]), w_dn=f(w_down[0]), cst=_host_consts())
    in_maps = []
    for c in range(NC_):
        m = dict(shared)
        m["xo"] = np.concatenate([x_prompt[0, NPR * c:NPR * (c + 1)], x_sample[2 * c], x_sample[2 * c + 1]], axis=0)
        m["xh"] = x_prompt[0, NPR * c - NHALO:NPR * c] if c > 0 else np.zeros((NHALO, D), np.float32)
        m["ck"] = cache_k[0, 2 * c:2 * c + 2].reshape(2, 512, D)
        m["cv"] = cache_v[0, 2 * c:2 * c + 2].reshape(2, 512, D)
        m["sconv"] = state_conv[0, 2 * c:2 * c + 2]
        m["sssm"] = state_ssm[0, 2 * c:2 * c + 2].reshape(2, 4096, 128)
        m["hones"] = np.full((128, 128), 1.0 if c > 0 else 0.0, np.float32)
        pm = np.zeros((64, NPRE), np.float32)
        pm[:, :min(c, NPRE)] = 1.0
        m["pmask"] = pm
        in_maps.append({k: np.ascontiguousarray(v) for k, v in m.items()})
    if DEBUG_CORES is None:
        res = run_bass_kernel_spmd(nc, in_maps, core_ids=list(range(NC_)))
        R = res.results
    else:
        res = run_bass_kernel_spmd(nc, [in_maps[c] for c in DEBUG_CORES], core_ids=list(range(len(DEBUG_CORES))), trace=DEBUG_TRACE)
        LAST_EXEC_NS[0] = res.exec_time_ns
        R = [res.results[DEBUG_CORES.index(c)] if c in DEBUG_CORES else res.results[0] for c in range(NC_)]
    y_p = np.concatenate([R[c]["y"][:NPR] for c in range(NC_)], axis=0)[None]
    y_s = np.stack([R[c]["y"][NPR + 64 * s:NPR + 64 * (s + 1)] for c in range(NC_) for s in range(2)], axis=0)
    conv_p = R[7]["convp"][None, None]
    ssm_p = R[7]["ssmp"].reshape(1, 1, 64, 64, 128)
    k_p = R[7]["kp"].reshape(1, 1, 512, 16, 128)
    v_p = R[7]["vp"].reshape(1, 1, 512, 16, 128)
    conv_s = np.stack([R[c]["convs"][s] for c in range(NC_) for s in range(2)], axis=0)[None]
    ssm_s = np.stack([R[c]["ssms"][s].reshape(64, 64, 128) for c in range(NC_) for s in range(2)], axis=0)[None]
    k_s = np.stack([R[c]["ks"][64 * s:64 * (s + 1)].reshape(64, 16, 128) for c in range(NC_) for s in range(2)], axis=0)[None]
    v_s = np.stack([R[c]["vs"][64 * s:64 * (s + 1)].reshape(64, 16, 128) for c in range(NC_) for s in range(2)], axis=0)[None]
    return tuple(np.ascontiguousarray(a.astype(np.float32)) for a in (y_p, y_s, conv_p, ssm_p, k_p, v_p, conv_s, ssm_s, k_s, v_s))
```

```python
import numpy as np
from contextlib import ExitStack
import concourse.bass as bass
import concourse.mybir as mybir
from concourse.bass_utils import run_bass_kernel_spmd

F32 = mybir.dt.float32
BF16 = mybir.dt.bfloat16
AF = mybir.ActivationFunctionType
ALU = mybir.AluOpType
AX = mybir.AxisListType

D = 2048
NC_ = 8
NPR = 1024
NOWN = 1152
NHALO = 512
INC = 20544
DFF = 5632
C_Z, C_X, C_B, C_C, C_DT, C_Q, C_K, C_V, C_GS, C_GA = 0, 4096, 8192, 9216, 10240, 10304, 12352, 14400, 16448, 18496
ALPHA = 2.0 ** 0.25
NL = 10
NPRE = 7
CINW = 1161
DO_PRE = True
DEBUG_ALLOC = False
DO_Y = True
STOP = 99
P1_STOP = 99
DEBUG_CORES = None
DEBUG_TRACE = False
LAST_EXEC_NS = [None]
NPRE_RUN = NPRE


class T:
    def __init__(self, h, excl=False):
        self.h = h
        self.w = None
        self.r = {}
        self.excl = excl

    def __getitem__(self, k):
        return self.h[k]


class TV:
    def __init__(self, base, h):
        object.__setattr__(self, "base", base)
        object.__setattr__(self, "h", h)

    def __getitem__(self, k):
        return self.h[k]

    def __getattr__(self, n):
        return getattr(self.base, n)

    def __setattr__(self, n, v):
        setattr(self.base, n, v)


class Sched:
    def __init__(self, nc, es):
        self.nc = nc
        self.es = es
        self.E = {}
        for name, h in (("pe", nc.tensor), ("act", nc.scalar), ("dve", nc.vector), ("pool", nc.gpsimd), ("sp", nc.sync)):
            sem = es.enter_context(nc.semaphore("e_" + name))
            self.E[name] = dict(h=h, sem=sem, cnt=0, waited={}, name=name)
        self.lanes = {}
        self.lptr = {}
        for q in ("sp", "pool"):
            self.lanes[q] = [dict(sem=es.enter_context(nc.semaphore(f"l_{q}{i}")), val=0, idx=i) for i in range(NL)]
            self.lptr[q] = 0
        self.nsb = 0

    def sb(self, shape, dt, es=None, name=None):
        self.nsb += 1
        h = (es or self.es).enter_context(self.nc.sbuf_tensor(f"{name or "sb"}_{self.nsb}", list(shape), dt))
        if DEBUG_ALLOC:
            print('alloc', name, shape, dt, 'remaining', self.nc.sbuf_bytes_remaining)
        return T(h)

    def _wait(self, E, evs):
        best = {}
        for ev in evs:
            k = id(ev[0])
            if k not in best or best[k][1] < ev[1]:
                best[k] = ev
        for k, ev in best.items():
            if E["waited"].get(k, 0) < ev[1]:
                E["h"].wait_ge(ev[0], ev[1])
                E["waited"][k] = ev[1]

    def _deps(self, eng, reads, writes):
        evs = []
        for t in reads:
            if t.w is not None and not (eng == "pe" and t.w[2] == "pe"):
                evs.append(t.w)
            if t.excl:
                for k, ev in t.r.items():
                    if k != eng:
                        evs.append(ev)
        for t in writes:
            if t.w is not None and not (eng == "pe" and t.w[2] == "pe"):
                evs.append(t.w)
            for ev in t.r.values():
                if not (eng == "pe" and ev[2] == "pe"):
                    evs.append(ev)
        return evs

    def op(self, eng, fn, reads=(), writes=(), sig=True):
        E = self.E[eng]
        self._wait(E, self._deps(eng, reads, writes))
        ins = fn(E["h"])
        if sig:
            E["cnt"] += 1
            ins.then_inc(E["sem"], 1)
            ev = (E["sem"], E["cnt"], eng)
        else:
            ev = (E["sem"], E["cnt"] + 1, eng)
        for t in reads:
            t.r[eng] = ev
        for t in writes:
            t.w = ev
            t.r = {}
        return ins

    def dma(self, q, out, in_, reads=(), writes=(), **kw):
        E = self.E[q]
        ln = self.lanes[q][self.lptr[q] % NL]
        self.lptr[q] += 1
        evs = self._deps(q, reads, writes)
        if ln["val"] > 0:
            evs.append((ln["sem"], ln["val"], "dma"))
        self._wait(E, evs)
        ins = E["h"].dma_start(out=out, in_=in_, **kw)
        ln["val"] += 16
        ins.then_inc(ln["sem"], 16)
        ev = (ln["sem"], ln["val"], "dma")
        for t in reads:
            t.r[(q, ln["idx"])] = ev
        for t in writes:
            t.w = ev
            t.r = {}

    def barrier(self, engs=("pe", "act", "dve", "pool", "sp")):
        evs = [(e["sem"], e["cnt"], n) for n, e in self.E.items() if e["cnt"] > 0]
        for q in self.lanes:
            for ln in self.lanes[q]:
                if ln["val"] > 0:
                    evs.append((ln["sem"], ln["val"], "dma"))
        for n in engs:
            self._wait(self.E[n], evs)


def build_program():
    nc = bass.Bass("TRN2", target_bir_lowering=False)

    def din(name, shape):
        return nc.dram_tensor(name, list(shape), F32, kind="ExternalInput").ap()

    def dout(name, shape):
        return nc.dram_tensor(name, list(shape), F32, kind="ExternalOutput").ap()

    xo = din("xo", [NOWN, D]); xh = din("xh", [NHALO, D]); xp = din("xp", [NPRE * NPR, D])
    ck = din("ck", [2, 512, D]); cv = din("cv", [2, 512, D])
    sconv = din("sconv", [2, 3, 6144]); sssm = din("sssm", [2, 4096, 128])
    w_in = din("w_in", [D, INC]); conv_w = din("conv_w", [4, 6144]); conv_b = din("conv_b", [6144])
    dt_bias = din("dt_bias", [64]); a_log = din("a_log", [64]); d_skip = din("d_skip", [64])
    norm_w = din("norm_w", [4096]); bt_in = din("bt", [3, 128, 16 * 128]); relc = din("relc", [16])
    w_so = din("w_so", [4096, D]); w_ao = din("w_ao", [D, D]); w_o = din("w_o", [D, D])
    ln1_g = din("ln1_g", [D]); ln1_b = din("ln1_b", [D]); ln2_g = din("ln2_g", [D]); ln2_b = din("ln2_b", [D])
    w_gu = din("w_gu", [D, 2 * DFF]); w_dn = din("w_dn", [DFF, D])
    cst = din("cst", [128, 512])
    hones_in = din("hones", [128, 128]); pmask_in = din("pmask", [128, NPRE])

    y_o = dout("y", [NOWN, D]); convp_o = dout("convp", [3, 6144]); ssmp_o = dout("ssmp", [4096, 128])
    kp_o = dout("kp", [512, D]); vp_o = dout("vp", [512, D])
    convs_o = dout("convs", [2, 3, 6144]); ssms_o = dout("ssms", [2, 4096, 128])
    ks_o = dout("ks", [128, D]); vs_o = dout("vs", [128, D])
    att_scr = nc.dram_tensor("att_scr", [128, 16, NOWN], BF16).ap()
    ssdy_scr = nc.dram_tensor("ssdy_scr", [8, 128, 4, NOWN], BF16).ap()
    ATTS = T(None)
    SSDYS = T(None)

    with ExitStack() as es:
        S = Sched(nc, es)
        psf = [T(es.enter_context(nc.psum_tensor(f"psf{i}", [128, 512], F32)), excl=True) for i in range(8)]
        psbv = [TV(t, t.h.bitcast(BF16)) for t in psf]
        pctr = [0, 0]
        pf_set = [[0, 1, 2, 3, 4, 5]]
        pb_set = [[6, 7]]

        def PF():
            pctr[0] += 1
            s_ = pf_set[0]
            return psf[s_[pctr[0] % len(s_)]]

        def PB():
            pctr[1] += 1
            s_ = pb_set[0]
            return psbv[s_[pctr[1] % len(s_)]]

        CF = S.sb([128, 512], F32)
        S.dma("sp", CF[:], cst[:, :], writes=[CF])
        identf = CF[:, 0:128]
        mle = CF[0:64, 128:192]
        lt = CF[0:64, 192:256]
        onesf = CF[0:64, 256:384]
        CB = S.sb([128, 384], BF16)
        S.dma("pool", CB[:, 0:256], cst[:, 0:256], writes=[CB])
        S.dma("pool", CB[:, 256:384], hones_in[:, :], writes=[CB])
        OB = S.sb([128, 128], BF16)
        S.op("dve", lambda e: e.memset(OB[:], 1.0), writes=[OB])
        identb = CB[:, 0:128]
        honesb = CB[:, 256:384]
        PM = S.sb([128, NPRE], F32)
        S.dma("sp", PM[:], pmask_in[:, :], writes=[PM])
        HV = S.sb([128, 4, 64], F32)
        S.dma("sp", HV[:, 0, :], dt_bias.partition_broadcast(128), writes=[HV])
        S.dma("sp", HV[:, 1, :], a_log.partition_broadcast(128), writes=[HV])
        S.dma("sp", HV[:, 2, :], d_skip.partition_broadcast(128), writes=[HV])
        S.op("act", lambda e: e.activation(out=HV[:, 1, :], in_=HV[:, 1, :], func=AF.Exp), reads=[HV], writes=[HV])
        S.op("dve", lambda e: e.tensor_scalar(out=HV[:, 1, :], in0=HV[:, 1, :], scalar1=-1.0, scalar2=None, op0=ALU.mult),
             reads=[HV], writes=[HV])
        CW = S.sb([128, 48, 4], F32)
        CBI = S.sb([128, 48], F32)
        with nc.allow_non_contiguous_dma(reason="small param layout"):
            for tap in range(4):
                S.dma("sp", CW[:, :, tap], conv_w[tap].rearrange("(t p) -> p t", p=128), writes=[CW])
            S.dma("sp", CBI[:, :], conv_b.rearrange("(t p) -> p t", p=128), writes=[CBI])
        SH = S.sb([128, 48, 2, 3], F32)
        with nc.allow_non_contiguous_dma(reason="conv state transpose load"):
            for s in range(2):
                for r in range(3):
                    S.dma("sp", SH[:, :, s, r], sconv[s, r].rearrange("(t p) -> p t", p=128), writes=[SH])

        NSLOT = 3
        WS = [S.sb([128, 16, 512], BF16, name=f"ws{i}") for i in range(NSLOT)]
        wctr = [0]

        def load_w(src, kt, ncol):
            wctr[0] += 1
            sl = WS[wctr[0] % NSLOT]
            S.dma("pool", sl[:, 0:kt, 0:ncol], src.rearrange("(k p) c -> p k c", p=128), writes=[sl])
            return sl

        XB = [None, None]
        xbc = [0]

        def make_xT(src_rows, dst, col0):
            xbc[0] += 1
            xb = XB[xbc[0] % 2]
            S.dma("pool", xb[:], src_rows, writes=[xb])
            for half in range(2):
                pb = PB()
                for j in range(8):
                    k = half * 8 + j
                    S.op("pe", lambda e, k=k, j=j, pb=pb: e.transpose(pb[:, j * 128:(j + 1) * 128], xb[:, k * 128:(k + 1) * 128], identb),
                         reads=[xb], writes=[pb], sig=(j == 7))
                eng = "act" if half == 0 else "dve"
                if eng == "act":
                    S.op("act", lambda e, pb=pb, half=half: e.copy(out=dst[:, half * 8:(half + 1) * 8, col0:col0 + 128],
                                                                     in_=pb[:, :].rearrange("p (k t) -> p k t", k=8)),
                         reads=[pb], writes=[dst])
                else:
                    S.op("dve", lambda e, pb=pb, half=half: e.tensor_copy(out=dst[:, half * 8:(half + 1) * 8, col0:col0 + 128],
                                                                            in_=pb[:, :].rearrange("p (k t) -> p k t", k=8)),
                         reads=[pb], writes=[dst])


        hcs = nc.dram_tensor("hcscratch", [128, 4096], F32).ap()
        HCS = T(None)

        def dt_chain(ps, dst_dt, dst_da, mask_ap=None):
            pass

        if STOP == 0:
            S.barrier()
            return nc
        if DO_PRE:
            with ExitStack() as es1:
                XB[0] = S.sb([128, D], BF16, es1, name="xb0p"); XB[1] = S.sb([128, D], BF16, es1, name="xb1p")
                HC = S.sb([128, 4096], F32, es1, name="hc")
                S.op("dve", lambda e: e.memset(HC[:], 0.0), writes=[HC])
                XTPs = [S.sb([128, 16, NPR], BF16, es1, name=f"xtp{i}") for i in range(1)]
                CINP = [S.sb([128, 3 + NPR], F32, es1, name=f"cinp{i}") for i in range(2)]
                HIST = S.sb([128, 40, 3], F32, es1, name="hist")
                S.op("dve", lambda e: e.memset(HIST[:], 0.0), writes=[HIST])
                XSBP = S.sb([128, 8, NPR], BF16, es1, name="xsbp")
                XSG = [S.sb([128, 4, NPR], BF16, es1, name=f"xsg{i}") for i in range(2)]
                ACC = [S.sb([128, NPR], F32, es1, name=f"accp{i}") for i in range(2)]
                DTPs = [S.sb([128, 8, 64], F32, es1, name=f"dtp{i}") for i in range(2)]
                DAPs = [S.sb([128, 8, 64], F32, es1, name=f"dap{i}") for i in range(2)]
                WDPs = [S.sb([128, 8, 64], F32, es1, name=f"wdp{i}") for i in range(2)]
                CDPs = [S.sb([128, 64], F32, es1, name=f"cdp{i}") for i in range(2)]
                TMP = [S.sb([128, 64], F32, es1, name=f"tmpp{i}") for i in range(2)]
                XWP = [S.sb([64, 512], BF16, es1, name=f"xwp{i}") for i in range(3)]
                BTP = [S.sb([64, 128], BF16, es1, name=f"btp{i}") for i in range(3)]
                HTMP = [S.sb([128, 512], F32, es1, name=f"htmp{i}") for i in range(2)]
                pf_set[0] = [0, 1, 2]
                PSACC = psf[3]
                pb_set[0] = [4, 5, 6, 7]

                def interleave(g1, g2, n1=1, n2=1):
                    a1 = a2 = True
                    while a1 or a2:
                        for _ in range(n1):
                            if a1:
                                try:
                                    next(g1)
                                except StopIteration:
                                    a1 = False
                        for _ in range(n2):
                            if a2:
                                try:
                                    next(g2)
                                except StopIteration:
                                    a2 = False
                        yield

                def run(g):
                    for _ in g:
                        pass

                def gen_A(blk):
                    XTP = XTPs[0]
                    DTP, DAP, WDP, CDP = DTPs[blk % 2], DAPs[blk % 2], WDPs[blk % 2], CDPs[blk % 2]
                    for i in range(8):
                        make_xT(xp[blk * NPR + i * 128: blk * NPR + (i + 1) * 128, :], XTP, i * 128)
                        yield
                    wdt = load_w(w_in[:, C_DT:C_DT + 64], 16, 64)
                    lt128 = CF[:, 384:512]
                    ones128 = CF[:, 256:384]
                    for c in range(8):
                        ps = PF()
                        for k in range(16):
                            S.op("pe", lambda e, k=k, c=c, ps=ps: e.matmul(ps[:, 0:64], lhsT=XTP[:, k, c * 128:(c + 1) * 128], rhs=wdt[:, k, 0:64],
                                                                           start=(k == 0), stop=(k == 15)),
                                 reads=[XTP, wdt], writes=[ps], sig=(k == 15))
                        tm = TMP[c % 2]
                        S.op("dve", lambda e, ps=ps, tm=tm: e.tensor_tensor(out=tm[:], in0=ps[:, 0:64], in1=HV[:, 0, :], op=ALU.add),
                             reads=[ps, HV], writes=[tm])
                        S.op("act", lambda e, tm=tm: e.activation(out=tm[:], in_=tm[:], func=AF.Exp), reads=[tm], writes=[tm])
                        S.op("act", lambda e, tm=tm, c=c: e.activation(out=DTP[:, c, :], in_=tm[:], func=AF.Ln, bias=1.0, scale=1.0),
                             reads=[tm], writes=[DTP])
                        S.op("dve", lambda e, c=c: e.tensor_scalar(out=DTP[:, c, :], in0=DTP[:, c, :], scalar1=PM[:, blk:blk + 1], scalar2=None,
                                                                   op0=ALU.mult), reads=[DTP, PM], writes=[DTP])
                        S.op("dve", lambda e, c=c: e.tensor_tensor(out=DAP[:, c, :], in0=DTP[:, c, :], in1=HV[:, 1, :], op=ALU.mult),
                             reads=[DTP, HV], writes=[DAP])
                        yield
                    for c in range(8):
                        ps2 = PF()
                        S.op("pe", lambda e, c=c, ps2=ps2: e.matmul(ps2[:, 0:64], lhsT=lt128, rhs=DAP[:, c, :], start=True, stop=(c == 7)),
                             reads=[DAP, CF], writes=[ps2], sig=(c == 7))
                        for c2 in range(c + 1, 8):
                            S.op("pe", lambda e, c2=c2, ps2=ps2: e.matmul(ps2[:, 0:64], lhsT=ones128, rhs=DAP[:, c2, :], start=False, stop=(c2 == 7)),
                                 reads=[DAP, CF], writes=[ps2], sig=(c2 == 7))
                        S.op("act", lambda e, c=c, ps2=ps2: e.activation(out=WDP[:, c, :], in_=ps2[:, 0:64], func=AF.Exp), reads=[ps2], writes=[WDP])
                        S.op("dve", lambda e, c=c: e.tensor_tensor(out=WDP[:, c, :], in0=WDP[:, c, :], in1=DTP[:, c, :], op=ALU.mult),
                             reads=[WDP, DTP], writes=[WDP])
                        yield
                    ps3 = PF()
                    for c in range(8):
                        S.op("pe", lambda e, c=c, ps3=ps3: e.matmul(ps3[:, 0:64], lhsT=ones128, rhs=DAP[:, c, :], start=(c == 0), stop=(c == 7)),
                             reads=[DAP, CF], writes=[ps3], sig=(c == 7))
                    S.op("act", lambda e, ps3=ps3: e.activation(out=CDP[:, :], in_=ps3[:, 0:64], func=AF.Exp), reads=[ps3], writes=[CDP])
                    yield

                def proj_tile(blk, wb, ct, wsl):
                    XTP = XTPs[0]
                    xsg = XSG[wb % 2]
                    ft = wb * 4 + ct
                    cin = CINP[ft % 2]
                    S.op("act", lambda e: e.copy(out=cin[:, 0:3], in_=HIST[:, ft, :]), reads=[HIST], writes=[cin])
                    for tg in range(2):
                        ps = PF()
                        for k in range(16):
                            S.op("pe", lambda e, k=k, ps=ps, tg=tg: e.matmul(ps[:, 0:512], lhsT=wsl[:, k, ct * 128:(ct + 1) * 128],
                                                                          rhs=XTP[:, k, tg * 512:(tg + 1) * 512], start=(k == 0), stop=(k == 15)),
                                 reads=[wsl, XTP], writes=[ps], sig=(k == 15))
                        S.op("act", lambda e, ps=ps, tg=tg: e.copy(out=cin[:, 3 + tg * 512:3 + (tg + 1) * 512], in_=ps[:, 0:512]),
                             reads=[ps], writes=[cin])
                    S.op("act", lambda e: e.copy(out=HIST[:, ft, :], in_=cin[:, NPR:NPR + 3]), reads=[cin], writes=[HIST])
                    acc = ACC[ft % 2]
                    S.op("dve", lambda e: e.tensor_scalar(out=acc[:], in0=cin[:, 0:NPR], scalar1=CW[:, ft, 0:1], scalar2=CBI[:, ft:ft + 1], op0=ALU.mult, op1=ALU.add),
                         reads=[cin, CW, CBI], writes=[acc])
                    for tap in range(1, 4):
                        S.op("dve", lambda e, tap=tap: e.scalar_tensor_tensor(
                            out=acc[:], in0=cin[:, tap:tap + NPR], scalar=CW[:, ft, tap:tap + 1], in1=acc[:], op0=ALU.mult, op1=ALU.add),
                            reads=[cin, CW, acc], writes=[acc])
                    xdst, xdi = (XSBP, ft - 32) if wb >= 8 else (xsg, ct)
                    S.op("act", lambda e: e.activation(out=xdst[:, xdi, :], in_=acc[:], func=AF.Silu), reads=[acc], writes=[xdst])

                NXW = 4
                XWP8 = [S.sb([128, 512], BF16, es1, name=f"xwq{i}") for i in range(NXW)]
                BTP8 = [S.sb([128, 128], BF16, es1, name=f"btq{i}") for i in range(NXW)]

                def chunk_T(blk, g, c):
                    WDP = WDPs[blk % 2]
                    xsg = XSG[g % 2]
                    pb = PB()
                    for j in range(4):
                        S.op("pe", lambda e, j=j: e.transpose(pb[:, j * 128:(j + 1) * 128], xsg[:, j, c * 128:(c + 1) * 128], identb),
                             reads=[xsg], writes=[pb], sig=False)
                    S.op("pe", lambda e: e.transpose(pb[:, 512:640], XSBP[:, g, c * 128:(c + 1) * 128], identb), reads=[XSBP], writes=[pb])
                    xw = XWP8[c % NXW]
                    bt_ = BTP8[c % NXW]
                    S.op("dve", lambda e: e.tensor_tensor(
                        out=xw[:, :].rearrange("p (h q) -> p h q", h=8), in0=pb[:, 0:512].rearrange("p (h q) -> p h q", h=8),
                        in1=WDP[:, c, g * 8:(g + 1) * 8].rearrange("p (h o) -> p h o", o=1).to_broadcast([128, 8, 64]), op=ALU.mult),
                        reads=[pb, WDP], writes=[xw])
                    S.op("act", lambda e: e.copy(out=bt_[:], in_=pb[:, 512:640]), reads=[pb], writes=[bt_])

                def chunk_M(blk, g, c):
                    CDP = CDPs[blk % 2]
                    xw = XWP8[c % NXW]
                    bt_ = BTP8[c % NXW]
                    S.op("pe", lambda e: e.matmul(PSACC[:, 0:512], lhsT=bt_[:], rhs=xw[:], start=(c == 0), stop=(c == 7)),
                         reads=[xw, bt_], writes=[PSACC])
                    if c == 7:
                        ht = HTMP[g % 2]
                        S.op("dve", lambda e: e.tensor_tensor(
                            out=ht[:, :].rearrange("p (h q) -> p h q", h=8), in0=HC[:, g * 512:(g + 1) * 512].rearrange("p (h q) -> p h q", h=8),
                            in1=CDP[:, g * 8:(g + 1) * 8].rearrange("p (h o) -> p h o", o=1).to_broadcast([128, 8, 64]), op=ALU.mult),
                            reads=[HC, CDP], writes=[ht])
                        S.op("dve", lambda e: e.tensor_tensor(out=HC[:, g * 512:(g + 1) * 512], in0=ht[:], in1=PSACC[:, 0:512], op=ALU.add),
                             reads=[ht, PSACC], writes=[HC])

                def do_B(blk):
                    for wb in (8, 9, 0):
                        wsl = load_w(w_in[:, C_X + wb * 512:C_X + (wb + 1) * 512], 16, 512)
                        for ct in range(4):
                            proj_tile(blk, wb, ct, wsl)
                    for g in range(1, 8):
                        wsl = load_w(w_in[:, C_X + g * 512:C_X + (g + 1) * 512], 16, 512)
                        for q in range(4):
                            for c in range(2 * q, 2 * q + 2):
                                chunk_T(blk, g - 1, c)
                            proj_tile(blk, g, q, wsl)
                            for c in range(2 * q, 2 * q + 2):
                                chunk_M(blk, g - 1, c)
                    nxt = gen_A(blk + 1) if blk + 1 < NPRE_RUN else iter(())
                    for q in range(4):
                        for c in range(2 * q, 2 * q + 2):
                            chunk_T(blk, 7, c)
                        for _ in range(11):
                            next(nxt, None)
                        for c in range(2 * q, 2 * q + 2):
                            chunk_M(blk, 7, c)
                    run(nxt)

                run(gen_A(0))
                for blk in range(NPRE_RUN):
                    do_B(blk)
                S.dma("sp", hcs[:, :], HC[:], reads=[HC], writes=[HCS])
                S.barrier()
                pf_set[0] = [0, 1, 2, 3, 4, 5]
                pb_set[0] = [6, 7]

        if STOP == 1:
            S.barrier()
            return nc
        XH3 = S.sb([128, 16, 3], BF16, name="xh3")
        esxo = ExitStack()
        XTO = S.sb([128, 16, NOWN], BF16, esxo, name="xto")
        esa = ExitStack()
        ATT = S.sb([128, 16, NOWN], BF16, esa, name="att")
        esh = ExitStack()
        XTH = S.sb([128, 16, NHALO], BF16, esh, name="xth")
        with ExitStack() as esx:
            XB[0] = S.sb([128, D], BF16, esx, name="xb0m"); XB[1] = S.sb([128, D], BF16, esx, name="xb1m")
            for i in range(4):
                make_xT(xh[i * 128:(i + 1) * 128, :], XTH, i * 128)
            for i in range(9):
                make_xT(xo[i * 128:(i + 1) * 128, :], XTO, i * 128)
            S.barrier()
        if STOP == 2:
            S.barrier()
            return nc
        with ExitStack() as es2:
            BT = S.sb([128, 3, 4, 128], F32, es2, name="btb")
            RC = S.sb([128, 16], F32, es2, name="rc")
            S.dma("sp", RC[:], relc.partition_broadcast(128), writes=[RC])
            QT = S.sb([128, 4, NOWN], BF16, es2, name="qt")
            KT = S.sb([128, 4, 13 * 128], BF16, es2, name="kt")
            KTC = S.sb([128, 4, 512], BF16, es2, name="ktc")
            VB = S.sb([128, 13, 512], BF16, es2, name="vb")
            CVB = S.sb([128, 4, 512], BF16, es2, name="cvb")
            CKB = [S.sb([128, 512], BF16, es2, name=f"ckb{i}") for i in range(2)]
            KB = [S.sb([128, 512], BF16, es2, name=f"kb{i}") for i in range(2)]
            KF = [S.sb([128, 512], F32, es2, name=f"kf{i}") for i in range(2)]
            PT = [S.sb([128, 5, 64], BF16, es2, name=f"pt{i}") for i in range(2)]
            SB_ = [S.sb([128, 128], F32, es2, name=f"sbias{i}") for i in range(2)]
            RD = [S.sb([128, 64], F32, es2, name=f"rd{i}") for i in range(2)]
            kfc = [0]
            for hq in range(4):
                for v in range(3):
                    S.dma("sp", BT[:, v, :, :], bt_in[v].rearrange("p (h x) -> p h x", h=16)[:, hq * 4:(hq + 1) * 4, :], writes=[BT])
                for v in range(3):
                    S.op("dve", lambda e, v=v, hq=hq: e.tensor_tensor(out=BT[:, v, :, :], in0=BT[:, v, :, :],
                                                               in1=RC[:, hq * 4:(hq + 1) * 4].rearrange("p (h o) -> p h o", o=1).to_broadcast([128, 4, 128]), op=ALU.subtract),
                         reads=[BT, RC], writes=[BT])
                wq = load_w(w_in[:, C_Q + hq * 512:C_Q + (hq + 1) * 512], 16, 512)
                wk = load_w(w_in[:, C_K + hq * 512:C_K + (hq + 1) * 512], 16, 512)
                wv = load_w(w_in[:, C_V + hq * 512:C_V + (hq + 1) * 512], 16, 512)
                for hh in range(4):
                    for tg in range(3):
                        ps = PF()
                        for k in range(16):
                            S.op("pe", lambda e, k=k, ps=ps, hh=hh, tg=tg: e.matmul(ps[:, 0:384], lhsT=wq[:, k, hh * 128:(hh + 1) * 128],
                                                                                  rhs=XTO[:, k, tg * 384:(tg + 1) * 384], start=(k == 0), stop=(k == 15)),
                                 reads=[wq, XTO], writes=[ps], sig=(k == 15))
                        S.op("act", lambda e, ps=ps, hh=hh, tg=tg: e.activation(out=QT[:, hh, tg * 384:(tg + 1) * 384], in_=ps[:, 0:384], func=AF.Copy,
                                                                              scale=float(128 ** -0.5)), reads=[ps], writes=[QT])
                for a in range(13):
                    src, c0 = (XTH, a * 128) if a < 4 else (XTO, (a - 4) * 128)
                    for which, wsl in (("k", wk), ("v", wv)):
                        ps = PF()
                        for k in range(16):
                            S.op("pe", lambda e, k=k, ps=ps, src=src, c0=c0, wsl=wsl: e.matmul(ps[:, 0:512], lhsT=src[:, k, c0:c0 + 128], rhs=wsl[:, k, 0:512],
                                                                                             start=(k == 0), stop=(k == 15)),
                                 reads=[src, wsl], writes=[ps], sig=(k == 15))
                        if a >= 8:
                            kfc[0] += 1
                            kf = KF[kfc[0] % 2]
                            S.op("dve", lambda e, ps=ps, kf=kf: e.tensor_copy(out=kf[:], in_=ps[:, 0:512]), reads=[ps], writes=[kf])
                            if a < 12:
                                dst = (kp_o if which == "k" else vp_o)[(a - 8) * 128:(a - 7) * 128, hq * 512:(hq + 1) * 512]
                            else:
                                dst = (ks_o if which == "k" else vs_o)[:, hq * 512:(hq + 1) * 512]
                            S.dma("sp", dst, kf[:], reads=[kf])
                        if which == "v":
                            S.op("act", lambda e, ps=ps, a=a: e.copy(out=VB[:, a, :], in_=ps[:, 0:512]), reads=[ps], writes=[VB])
                        else:
                            kb = KB[a % 2]
                            S.op("act", lambda e, ps=ps, kb=kb: e.copy(out=kb[:], in_=ps[:, 0:512]), reads=[ps], writes=[kb])
                            pb = PB()
                            for hh in range(4):
                                S.op("pe", lambda e, pb=pb, kb=kb, hh=hh: e.transpose(pb[:, hh * 128:(hh + 1) * 128], kb[:, hh * 128:(hh + 1) * 128], identb),
                                     reads=[kb], writes=[pb], sig=(hh == 3))
                            S.op("dve", lambda e, pb=pb, a=a: e.tensor_copy(out=KT[:, :, a * 128:(a + 1) * 128],
                                                                            in_=pb[:, 0:512].rearrange("p (h t) -> p h t", h=4)), reads=[pb], writes=[KT])
                def prep_cache(s):
                    S.dma("pool", CVB[:, :, :], cv[s, :, hq * 512:(hq + 1) * 512].rearrange("(t p) c -> p t c", p=128), writes=[CVB])
                    for t in range(4):
                        ckb = CKB[t % 2]
                        S.dma("pool", ckb[:], ck[s, t * 128:(t + 1) * 128, hq * 512:(hq + 1) * 512], writes=[ckb])
                        pb = PB()
                        for hh in range(4):
                            S.op("pe", lambda e, pb=pb, ckb=ckb, hh=hh: e.transpose(pb[:, hh * 128:(hh + 1) * 128], ckb[:, hh * 128:(hh + 1) * 128], identb),
                                 reads=[ckb], writes=[pb], sig=(hh == 3))
                        S.op("dve", lambda e, pb=pb, s=s, t=t: e.tensor_copy(out=KTC[:, :, t * 128:(t + 1) * 128],
                                                                             in_=pb[:, 0:512].rearrange("p (h t) -> p h t", h=4)), reads=[pb], writes=[KTC])
                order = [(hh, c) for hh in range(4) for c in range(16)] + [(hh, 16 + s) for s in range(2) for hh in range(4)]

                def stage_S(n_it, hh, c):
                    pf_set[0] = [0, 1]
                    tiles = []
                    if c < 16:
                        par = c % 2
                        var = par
                        a_lo = c // 2
                        for t in range(5):
                            a = a_lo + t
                            r0, r1 = 0, 128
                            if par == 0 and t == 4:
                                r1 = 64
                            if par == 1 and t == 0:
                                r0 = 64
                            on = honesb if a < 4 else OB[:, :]
                            tiles.append((KT[:, hh, a * 128:(a + 1) * 128], VB[:, a, hh * 128:(hh + 1) * 128], r0, r1, on, [KT, VB]))
                        q0 = c * 64
                    else:
                        s = c - 16
                        var = 0 if s == 0 else 2
                        for t in range(4):
                            tiles.append((KTC[:, hh, t * 128:(t + 1) * 128], CVB[:, t, hh * 128:(hh + 1) * 128], 0, 128, OB[:, :], [KTC, CVB]))
                        tiles.append((KT[:, hh, 12 * 128:13 * 128], VB[:, 12, hh * 128:(hh + 1) * 128], 64 * s, 64 * s + 64, OB[:, :], [KT, VB]))
                        q0 = NPR + 64 * s
                    ps = PF()
                    for t, (kap, vap, r0, r1, on, rd) in enumerate(tiles):
                        S.op("pe", lambda e, kap=kap, t=t: e.matmul(ps[:, t * 64:(t + 1) * 64], lhsT=kap, rhs=QT[:, hh, q0:q0 + 64], start=True, stop=True),
                             reads=rd + [QT], writes=[ps], sig=(t == 4))
                    pt = PT[n_it % 2]
                    sbias = SB_[n_it % 2]
                    S.op("act", lambda e: e.activation(out=pt[:, 0:3, :], in_=ps[:, 0:192].rearrange("p (t q) -> p t q", t=3), func=AF.Exp), reads=[ps], writes=[pt])
                    S.op("dve", lambda e: e.tensor_tensor(out=sbias[:], in0=ps[:, 192:320], in1=BT[:, var, hh, :], op=ALU.add), reads=[ps, BT], writes=[sbias])
                    S.op("act", lambda e: e.activation(out=pt[:, 3:5, :], in_=sbias[:, :].rearrange("p (t q) -> p t q", t=2), func=AF.Exp), reads=[sbias], writes=[pt])
                    return (n_it, hh, tiles, q0, pt)

                def stage_V(ctx):
                    n_it, hh, tiles, q0, pt = ctx
                    hg = hq * 4 + hh
                    pf_set[0] = [2, 3, 4, 5]
                    po = PF()
                    pd = PF()
                    for t, (kap, vap, r0, r1, on, rd) in enumerate(tiles):
                        S.op("pe", lambda e, vap=vap, t=t, r0=r0, r1=r1: e.matmul(po[:, 0:64], lhsT=vap[r0:r1, :], rhs=pt[r0:r1, t, :], start=(t == 0), stop=(t == 4)),
                             reads=rd + [pt], writes=[po], sig=(t == 4))
                    for t, (kap, vap, r0, r1, on, rd) in enumerate(tiles):
                        S.op("pe", lambda e, on=on, t=t, r0=r0, r1=r1: e.matmul(pd[:, 0:64], lhsT=on[r0:r1, :], rhs=pt[r0:r1, t, :], start=(t == 0), stop=(t == 4)),
                             reads=[pt, CB, OB], writes=[pd], sig=(t == 4))
                    rdn = RD[n_it % 2]
                    S.op("dve", lambda e: e.reciprocal(out=rdn[:], in_=pd[:, 0:64]), reads=[pd], writes=[rdn])
                    S.op("dve", lambda e: e.tensor_tensor(out=ATT[:, hg, q0:q0 + 64], in0=po[:, 0:64], in1=rdn[:], op=ALU.mult), reads=[po, rdn], writes=[ATT])

                pend = None
                for n_it, (hh, c) in enumerate(order):
                    need_prep = (c >= 16 and hh == 0)
                    if need_prep:
                        if pend is not None:
                            stage_V(pend)
                            pend = None
                        prep_cache(c - 16)
                    ctx = stage_S(n_it, hh, c)
                    if pend is not None:
                        stage_V(pend)
                    pend = ctx
                stage_V(pend)
                pf_set[0] = [0, 1, 2, 3, 4, 5]
            S.barrier()

        if STOP == 3:
            S.barrier()
            return nc
        S.op("dve", lambda e: e.tensor_copy(out=XH3[:, :, :], in_=XTH[:, :, 509:512]), reads=[XTH], writes=[XH3])
        S.dma("sp", att_scr[:, :, :], ATT[:, :, :], reads=[ATT], writes=[ATTS])
        S.barrier()
        esh.close()
        esa.close()
        with ExitStack() as es3:
            NWG = [S.sb([64, 512], F32, es3, name=f"nwg{i}") for i in range(2)]
            SSDYG = [S.sb([128, 4, NOWN], BF16, es3, name=f"ssdyg{i}") for i in range(2)]
            DT = S.sb([64, 18, 64], F32, es3, name="dt")
            DA = S.sb([64, 18, 64], F32, es3, name="da")
            ECS = S.sb([64, 18, 64], F32, es3, name="ecs")
            WDT = S.sb([64, 18, 64], F32, es3, name="wdt")
            CDEC = S.sb([128, 18, 64], F32, es3, name="cdec")
            TMP = [S.sb([64, 64], F32, es3, name=f"tmpm{i}") for i in range(2)]
            wdt_w = load_w(w_in[:, C_DT:C_DT + 64], 16, 64)
            for c in range(18):
                t0 = c * 64
                ps = PF()
                for k in range(16):
                    S.op("pe", lambda e, k=k, ps=ps, t0=t0: e.matmul(ps[0:64, 0:64], lhsT=XTO[:, k, t0:t0 + 64], rhs=wdt_w[:, k, 0:64], start=(k == 0), stop=(k == 15)),
                         reads=[XTO, wdt_w], writes=[ps], sig=(k == 15))
                tm = TMP[c % 2]
                S.op("dve", lambda e, ps=ps, tm=tm: e.tensor_tensor(out=tm[:], in0=ps[0:64, 0:64], in1=HV[0:64, 0, :], op=ALU.add), reads=[ps, HV], writes=[tm])
                S.op("act", lambda e, tm=tm: e.activation(out=tm[:], in_=tm[:], func=AF.Exp), reads=[tm], writes=[tm])
                S.op("act", lambda e, tm=tm, c=c: e.activation(out=DT[:, c, :], in_=tm[:], func=AF.Ln, bias=1.0, scale=1.0), reads=[tm], writes=[DT])
                S.op("dve", lambda e, c=c: e.tensor_tensor(out=DA[:, c, :], in0=DT[:, c, :], in1=HV[0:64, 1, :], op=ALU.mult), reads=[DT, HV], writes=[DA])
                ps2 = PF()
                S.op("pe", lambda e, c=c, ps2=ps2: e.matmul(ps2[0:64, 0:64], lhsT=mle, rhs=DA[:, c, :], start=True, stop=True), reads=[DA, CF], writes=[ps2])
                S.op("pe", lambda e, c=c, ps2=ps2: e.matmul(ps2[0:64, 64:128], lhsT=lt, rhs=DA[:, c, :], start=True, stop=True), reads=[DA, CF], writes=[ps2])
                S.op("pe", lambda e, c=c, ps2=ps2: e.matmul(ps2[:, 128:192], lhsT=onesf, rhs=DA[:, c, :], start=True, stop=True), reads=[DA, CF], writes=[ps2])
                S.op("act", lambda e, c=c, ps2=ps2: e.activation(out=ECS[:, c, :], in_=ps2[0:64, 0:64], func=AF.Exp), reads=[ps2], writes=[ECS])
                S.op("act", lambda e, c=c, ps2=ps2: e.activation(out=WDT[:, c, :], in_=ps2[0:64, 64:128], func=AF.Exp), reads=[ps2], writes=[WDT])
                S.op("act", lambda e, c=c, ps2=ps2: e.activation(out=CDEC[:, c, :], in_=ps2[:, 128:192], func=AF.Exp), reads=[ps2], writes=[CDEC])
                S.op("dve", lambda e, c=c: e.tensor_tensor(out=WDT[:, c, :], in0=WDT[:, c, :], in1=DT[:, c, :], op=ALU.mult), reads=[WDT, DT], writes=[WDT])

            CIN = [S.sb([128, CINW], F32, es3, name=f"cin{i}") for i in range(2)]
            ACC = [S.sb([128, CINW - 3], F32, es3, name=f"acc{i}") for i in range(1)]
            XS = [S.sb([128, CINW - 3], BF16, es3, name=f"xs{i}") for i in range(6)]
            XD = [S.sb([64, (512 if DO_Y else 2)], BF16, es3, name=f"xd{i}") for i in range(2)]
            XW = [S.sb([64, 512], BF16, es3, name=f"xw{i}") for i in range(2)]
            XTS = [S.sb([64, 512], BF16, es3, name=f"xts{i}") for i in range(2)]
            DSK = S.sb([64, 8, 64], BF16, es3, name="dsk")
            BTK = [S.sb([64, 128], BF16, es3, name=f"btk{i}") for i in range(2)]
            SZ = [S.sb([64, 512], BF16, es3, name=f"sz{i}") for i in range(2)]
            CBM = [S.sb([64, (64 if DO_Y else 2)], F32, es3, name=f"cbm{i}") for i in range(2)]
            RR = [S.sb([64, (512 if DO_Y else 2)], F32, es3, name=f"rr{i}") for i in range(1)]
            EE = [S.sb([64, (512 if DO_Y else 2)], F32, es3, name=f"ee{i}") for i in range(1)]
            MT = [S.sb([64, (512 if DO_Y else 2)], BF16, es3, name=f"mt{i}") for i in range(2)]
            Y1 = [S.sb([64, (512 if DO_Y else 2)], F32, es3, name=f"y1{i}") for i in range(2)]
            Y2 = [S.sb([64, (512 if DO_Y else 2)], F32, es3, name=f"y2{i}") for i in range(1)]
            GO = [S.sb([64, (512 if DO_Y else 2)], BF16, es3, name=f"go{i}") for i in range(2)]
            SSQ = [S.sb([64, 2], F32, es3, name=f"ssq{i}") for i in range(2)]
            ZS = S.sb([128, 4, NOWN], BF16, es3, name="zs")
            HT = S.sb([128, 512], F32, es3, name="ht")
            HTB = S.sb([128, 512], BF16, es3, name="htb")
            HTM = S.sb([128, 512], F32, es3, name="htm")
            SIN = S.sb([128, 4, 128], F32, es3, name="sin")
            SOUT = [S.sb([128, 4, 128], F32, es3, name=f"sout{i}") for i in range(1)]
            soc = [0]

            def state_out(dst_rows):
                soc[0] += 1
                so = SOUT[0]
                ps = PF()
                for j in range(4):
                    S.op("pe", lambda e, ps=ps, j=j: e.transpose(ps[:, j * 128:(j + 1) * 128], HT[:, j * 128:(j + 1) * 128], identf),
                         reads=[HT, CF], writes=[ps], sig=(j == 3))
                S.op("act", lambda e, ps=ps, so=so: e.copy(out=so[:, :, :], in_=ps[:, 0:512].rearrange("p (j n) -> p j n", j=4)), reads=[ps], writes=[so])
                S.dma("sp", dst_rows.rearrange("(j p) n -> p j n", p=128), so[:, :, :], reads=[so])

            it = 0
            def load_group_w(g):
                wx_ = load_w(w_in[:, C_X + g * 512:C_X + (g + 1) * 512], 16, 512)
                wbc_ = load_w(w_in[:, C_B + g * 128:C_B + (g + 1) * 128], 16, 128)
                S.dma("pool", wbc_[:, 0:16, 128:256], w_in[:, C_C + g * 128:C_C + (g + 1) * 128].rearrange("(k p) c -> p k c", p=128), writes=[wbc_])
                wz_ = load_w(w_in[:, C_Z + g * 512:C_Z + (g + 1) * 512], 16, 512)
                return wz_, wx_, wbc_

            gw = load_group_w(0)
            for g in range(8):
                wz, wx, wbc = gw
                nwg = NWG[g % 2]
                ssdyg = SSDYG[g % 2]
                S.dma("sp", nwg[:], norm_w[g * 512:(g + 1) * 512].partition_broadcast(64), writes=[nwg])
                S.op("dve", lambda e, g=g: e.tensor_tensor(out=DSK[:, :, :], in0=identb[0:64, 0:64].rearrange("p (o q) -> p o q", o=1).to_broadcast([64, 8, 64]),
                                                         in1=HV[0:64, 2, g * 8:(g + 1) * 8].rearrange("p (h o) -> p h o", o=1).to_broadcast([64, 8, 64]), op=ALU.mult),
                     reads=[CB, HV], writes=[DSK])
                for fi in range(6):
                    if fi < 4:
                        wsl, wc0, ftg = wx, fi * 128, g * 4 + fi
                    elif fi == 4:
                        wsl, wc0, ftg = wbc, 0, 32 + g
                    else:
                        wsl, wc0, ftg = wbc, 128, 40 + g
                    cin = CIN[fi % 2]
                    segs = [(XH3, 0, 3, 0), (XTO, 0, 512, 3), (XTO, 512, 512, 515), (XTO, 1024, 64, 1030), (XTO, 1088, 64, 1097)]
                    for (src, c0, n, d0) in segs:
                        ps = PF()
                        for k in range(16):
                            S.op("pe", lambda e, k=k, ps=ps, src=src, c0=c0, n=n, wsl=wsl, wc0=wc0: e.matmul(ps[:, 0:n], lhsT=wsl[:, k, wc0:wc0 + 128],
                                                                                                         rhs=src[:, k, c0:c0 + n], start=(k == 0), stop=(k == 15)),
                                 reads=[wsl, src], writes=[ps], sig=(k == 15))
                        S.op("act", lambda e, ps=ps, cin=cin, n=n, d0=d0: e.copy(out=cin[:, d0:d0 + n], in_=ps[:, 0:n]), reads=[ps], writes=[cin])
                    for s in range(2):
                        S.op("dve", lambda e, cin=cin, s=s, ftg=ftg: e.tensor_copy(out=cin[:, 1027 + 67 * s:1030 + 67 * s], in_=SH[:, ftg, s, :]),
                             reads=[SH], writes=[cin])
                    with nc.allow_non_contiguous_dma(reason="conv state rows"):
                        S.dma("sp", convp_o[:, ftg * 128:(ftg + 1) * 128].rearrange("r f -> f r"), cin[:, 1024:1027], reads=[cin])
                        for s in range(2):
                            S.dma("sp", convs_o[s, :, ftg * 128:(ftg + 1) * 128].rearrange("r f -> f r"), cin[:, 1091 + 67 * s:1094 + 67 * s], reads=[cin])
                    acc = ACC[0]
                    W_ = CINW - 3
                    S.op("dve", lambda e, cin=cin, acc=acc, ftg=ftg: e.tensor_scalar(out=acc[:], in0=cin[:, 0:W_], scalar1=CW[:, ftg, 0:1], scalar2=CBI[:, ftg:ftg + 1],
                                                                                  op0=ALU.mult, op1=ALU.add), reads=[cin, CW, CBI], writes=[acc])
                    for tap in range(1, 4):
                        S.op("dve", lambda e, cin=cin, acc=acc, ftg=ftg, tap=tap: e.scalar_tensor_tensor(
                            out=acc[:], in0=cin[:, tap:tap + W_], scalar=CW[:, ftg, tap:tap + 1], in1=acc[:], op0=ALU.mult, op1=ALU.add),
                            reads=[cin, CW, acc], writes=[acc])
                    S.op("act", lambda e, acc=acc, fi=fi: e.activation(out=XS[fi][:], in_=acc[:], func=AF.Silu), reads=[acc], writes=[XS[fi]])
                for ct in range(4):
                    for tg in range(3):
                        ps = PF()
                        for k in range(16):
                            S.op("pe", lambda e, k=k, ps=ps, ct=ct, tg=tg: e.matmul(ps[:, 0:384], lhsT=wz[:, k, ct * 128:(ct + 1) * 128], rhs=XTO[:, k, tg * 384:(tg + 1) * 384],
                                                                                start=(k == 0), stop=(k == 15)), reads=[wz, XTO], writes=[ps], sig=(k == 15))
                        S.op("act", lambda e, ps=ps, ct=ct, tg=tg: e.activation(out=ZS[:, ct, tg * 384:(tg + 1) * 384], in_=ps[:, 0:384], func=AF.Silu), reads=[ps], writes=[ZS])
                if g + 1 < 8:
                    gw = load_group_w(g + 1)

                def stage_T(c):
                    pf_set[0] = [0, 1]
                    pb_set[0] = [5, 6]
                    i2 = c % 2
                    if c < 16:
                        q0, t0 = c * 64, c * 64
                    else:
                        s = c - 16
                        q0, t0 = 1027 + 67 * s, NPR + 64 * s

                    def bch(src, c=c, g=g, np_=64):
                        return src[0:np_, c, g * 8:(g + 1) * 8].rearrange("p (h o) -> p h o", o=1).to_broadcast([np_, 8, 64])
                    xd, xw, xts, btk = XD[i2], XW[i2], XTS[i2], BTK[i2]
                    sz = SZ[i2]
                    mt = MT[i2]
                    if DO_Y:
                        rr = RR[0]
                        S.op("dve", lambda e, rr=rr: e.tensor_tensor(out=rr[:, :].rearrange("p (h q) -> p h q", h=8), in0=bch(DA),
                                                                     in1=mle.rearrange("p (o q) -> p o q", o=1).to_broadcast([64, 8, 64]), op=ALU.mult),
                             reads=[DA, CF], writes=[rr])
                    pb = PB()
                    for j in range(4):
                        S.op("pe", lambda e, pb=pb, j=j, q0=q0: e.transpose(pb[0:64, j * 128:(j + 1) * 128], XS[j][:, q0:q0 + 64], identb), reads=[XS[j]], writes=[pb], sig=False)
                    S.op("pe", lambda e, pb=pb, q0=q0: e.transpose(pb[0:64, 512:640], XS[4][:, q0:q0 + 64], identb), reads=[XS[4]], writes=[pb])
                    x3 = pb[0:64, 0:512].rearrange("p (h q) -> p h q", h=8)


                    if DO_Y:
                        S.op("dve", lambda e, xd=xd, x3=x3: e.tensor_tensor(out=xd[:, :].rearrange("p (h q) -> p h q", h=8), in0=x3, in1=bch(DT), op=ALU.mult),
                             reads=[pb, DT], writes=[xd])
                    if DO_Y:
                        S.op("act", lambda e, xts=xts, pb=pb: e.copy(out=xts[:], in_=pb[0:64, 0:512]), reads=[pb], writes=[xts])
                    S.op("pool", lambda e, xw=xw, xts=xts: e.tensor_tensor(out=xw[:, :].rearrange("p (h q) -> p h q", h=8), in0=xts[:, :].rearrange("p (h q) -> p h q", h=8),
                                                                    in1=bch(WDT), op=ALU.mult), reads=[xts, WDT], writes=[xw])
                    S.op("act", lambda e, pb=pb, btk=btk: e.copy(out=btk[:], in_=pb[0:64, 512:640]), reads=[pb], writes=[btk])
                    if DO_Y:
                        zb = PB()
                        for j in range(4):
                            S.op("pe", lambda e, zb=zb, j=j, t0=t0: e.transpose(zb[0:64, j * 128:(j + 1) * 128], ZS[:, j, t0:t0 + 64], identb), reads=[ZS], writes=[zb], sig=(j == 3))
                        S.op("act", lambda e, zb=zb, sz=sz: e.copy(out=sz[:], in_=zb[0:64, 0:512]), reads=[zb], writes=[sz])
                        pc = PF()
                        S.op("pe", lambda e, pc=pc, q0=q0: e.matmul(pc[0:64, 0:64], lhsT=XS[4][:, q0:q0 + 64], rhs=XS[5][:, q0:q0 + 64], start=True, stop=True),
                             reads=[XS[4], XS[5]], writes=[pc])
                        cbm = CBM[i2]
                        S.op("dve", lambda e, pc=pc, cbm=cbm: e.tensor_tensor(out=cbm[:], in0=pc[0:64, 0:64], in1=mle, op=ALU.mult), reads=[pc, CF], writes=[cbm])
                        rr, ee = RR[0], EE[0]
                        pd = PF()
                        S.op("pe", lambda e, pd=pd, rr=rr: e.matmul(pd[0:64, 0:512], lhsT=lt, rhs=rr[:], start=True, stop=True), reads=[rr, CF], writes=[pd])
                        S.op("act", lambda e, pd=pd, ee=ee: e.activation(out=ee[:], in_=pd[0:64, 0:512], func=AF.Exp), reads=[pd], writes=[ee])
                        S.op("pool", lambda e, ee=ee, mt=mt, cbm=cbm: e.tensor_tensor(
                            out=mt[:, :].rearrange("p (h q) -> p h q", h=8), in0=ee[:, :].rearrange("p (h q) -> p h q", h=8),
                            in1=cbm[:, :].rearrange("p (o q) -> p o q", o=1).to_broadcast([64, 8, 64]), op=ALU.mult), reads=[ee, cbm], writes=[mt])

                def stage_Y(c):
                    pf_set[0] = [2, 3, 4]
                    pb_set[0] = [7]
                    i2 = c % 2
                    if c < 16:
                        q0, t0 = c * 64, c * 64
                    else:
                        s = c - 16
                        q0, t0 = 1027 + 67 * s, NPR + 64 * s

                    def bch(src, c=c, g=g, np_=64):
                        return src[0:np_, c, g * 8:(g + 1) * 8].rearrange("p (h o) -> p h o", o=1).to_broadcast([np_, 8, 64])
                    xd, xw, xts, btk = XD[i2], XW[i2], XTS[i2], BTK[i2]
                    sz = SZ[i2]
                    mt = MT[i2]
                    if c == 0:
                        if DO_PRE:
                            S.dma("sp", HT[:], hcs[:, g * 512:(g + 1) * 512], reads=[HCS], writes=[HT])
                        else:
                            S.op("dve", lambda e: e.memset(HT[:], 0.0), writes=[HT])
                        S.op("act", lambda e: e.copy(out=HTB[:], in_=HT[:]), reads=[HT], writes=[HTB])
                    elif c >= 16:
                        s = c - 16
                        S.dma("sp", SIN[:, :, :], sssm[s, g * 512:(g + 1) * 512, :].rearrange("(j p) n -> p j n", p=128), writes=[SIN])
                        ps = PF()
                        for j in range(4):
                            S.op("pe", lambda e, ps=ps, j=j: e.transpose(ps[:, j * 128:(j + 1) * 128], SIN[:, j, :], identf), reads=[SIN, CF], writes=[ps], sig=(j == 3))
                        S.op("act", lambda e, ps=ps: e.copy(out=HT[:], in_=ps[:, 0:512]), reads=[ps], writes=[HT])
                        S.op("act", lambda e: e.copy(out=HTB[:], in_=HT[:]), reads=[HT], writes=[HTB])
                    if DO_Y:
                        po = PF()
                        S.op("pe", lambda e, po=po, q0=q0: e.matmul(po[0:64, 0:512], lhsT=XS[5][:, q0:q0 + 64], rhs=HTB[:], start=True, stop=True),
                             reads=[XS[5], HTB], writes=[po])
                        y1 = Y1[i2]
                        S.op("dve", lambda e, po=po, y1=y1: e.tensor_tensor(out=y1[:, :].rearrange("p (h q) -> p h q", h=8), in0=po[0:64, 0:512].rearrange("p (h q) -> p h q", h=8),
                                                                           in1=bch(ECS), op=ALU.mult), reads=[po, ECS], writes=[y1])
                    pst = PF()
                    S.op("pe", lambda e, pst=pst, btk=btk, xw=xw: e.matmul(pst[:, 0:512], lhsT=btk[:], rhs=xw[:], start=True, stop=True), reads=[btk, xw], writes=[pst])
                    S.op("dve", lambda e: e.tensor_tensor(out=HTM[:, :].rearrange("p (h q) -> p h q", h=8), in0=HT[:, :].rearrange("p (h q) -> p h q", h=8),
                                                          in1=bch(CDEC, np_=128), op=ALU.mult), reads=[HT, CDEC], writes=[HTM])
                    S.op("dve", lambda e, pst=pst: e.tensor_tensor(out=HT[:], in0=HTM[:], in1=pst[:, 0:512], op=ALU.add), reads=[HTM, pst], writes=[HT])
                    S.op("act", lambda e: e.copy(out=HTB[:], in_=HT[:]), reads=[HT], writes=[HTB])
                    if c == 15:
                        state_out(ssmp_o[g * 512:(g + 1) * 512, :])
                    elif c >= 16:
                        state_out(ssms_o[c - 16, g * 512:(g + 1) * 512, :])

                    if DO_Y:
                        py = PF()
                        for h in range(8):
                            S.op("pe", lambda e, py=py, h=h, mt=mt, xd=xd: e.matmul(py[0:64, h * 64:(h + 1) * 64], lhsT=mt[:, h * 64:(h + 1) * 64], rhs=xd[:, h * 64:(h + 1) * 64],
                                                                                  start=True, stop=False), reads=[mt, xd], writes=[py], sig=False)
                            S.op("pe", lambda e, py=py, h=h, xts=xts: e.matmul(py[0:64, h * 64:(h + 1) * 64], lhsT=DSK[:, h, :], rhs=xts[:, h * 64:(h + 1) * 64],
                                                                             start=False, stop=True), reads=[DSK, xts], writes=[py], sig=(h == 7))
                        y1, y2 = Y1[i2], Y2[0]
                        S.op("dve", lambda e, py=py, y1=y1, y2=y2: e.tensor_tensor(out=y2[:], in0=py[0:64, 0:512], in1=y1[:], op=ALU.add), reads=[py, y1], writes=[y2])
                        S.op("pool", lambda e, y2=y2, sz=sz, y1=y1: e.tensor_tensor(out=y1[:], in0=y2[:], in1=sz[:], op=ALU.mult), reads=[y2, sz], writes=[y1])
                        ssq = SSQ[i2]
                        S.op("act", lambda e, y1=y1, y2=y2, ssq=ssq: e.activation(out=y2[:], in_=y1[:], func=AF.Square, accum_out=ssq[:, 0:1]), reads=[y1], writes=[y2, ssq])
                        S.op("act", lambda e, ssq=ssq: e.activation(out=ssq[:, 1:2], in_=ssq[:, 0:1], func=AF.Ln, bias=1e-5, scale=1.0 / 512.0), reads=[ssq], writes=[ssq])
                        S.op("act", lambda e, ssq=ssq: e.activation(out=ssq[:, 1:2], in_=ssq[:, 1:2], func=AF.Exp, scale=-0.5), reads=[ssq], writes=[ssq])

                def stage_Y2(c):
                    pf_set[0] = [2, 3, 4]
                    pb_set[0] = [7]
                    i2 = c % 2
                    if c < 16:
                        q0, t0 = c * 64, c * 64
                    else:
                        s = c - 16
                        q0, t0 = 1027 + 67 * s, NPR + 64 * s

                    def bch(src, c=c, g=g, np_=64):
                        return src[0:np_, c, g * 8:(g + 1) * 8].rearrange("p (h o) -> p h o", o=1).to_broadcast([np_, 8, 64])
                    xd, xw, xts, btk = XD[i2], XW[i2], XTS[i2], BTK[i2]
                    sz = SZ[i2]
                    mt = MT[i2]
                    y1 = Y1[i2]
                    ssq = SSQ[i2]
                    if DO_Y:
                        go = GO[i2]
                        S.op("dve", lambda e, y1=y1, ssq=ssq, go=go, nwg=nwg: e.scalar_tensor_tensor(out=go[:], in0=y1[:], scalar=ssq[:, 1:2], in1=nwg[:],
                                                                                               op0=ALU.mult, op1=ALU.mult), reads=[y1, ssq, nwg], writes=[go])
                        pb2 = PB()
                        for j in range(4):
                            S.op("pe", lambda e, pb2=pb2, j=j, go=go: e.transpose(pb2[:, j * 64:(j + 1) * 64], go[:, j * 128:(j + 1) * 128], identb[0:64, 0:64]),
                                 reads=[go], writes=[pb2], sig=(j == 3))
                        S.op("act", lambda e, pb2=pb2, ssdyg=ssdyg, t0=t0: e.copy(out=ssdyg[:, :, t0:t0 + 64], in_=pb2[:, 0:256].rearrange("p (j t) -> p j t", j=4)),
                             reads=[pb2], writes=[ssdyg])

                stage_T(0)
                for c in range(18):
                    if c + 1 < 18:
                        stage_T(c + 1)
                    stage_Y(c)
                    if c >= 1:
                        stage_Y2(c - 1)
                stage_Y2(17)
                pf_set[0] = [0, 1, 2, 3, 4, 5]
                pb_set[0] = [6, 7]
                if DO_Y:
                    S.dma("sp", ssdy_scr[g], ssdyg[:, :, :], reads=[ssdyg], writes=[SSDYS])
            S.barrier()

        if STOP == 4:
            return nc
        FMAX = int(nc.vector.BN_STATS_FMAX)
        SDIM = int(nc.vector.BN_STATS_DIM)
        nst = (D + FMAX - 1) // FMAX
        assert D % nst == 0
        fch = D // nst

        def layer_norm_rows(src_ap, dst_ap, G, Bb, ST, MV, rd_tiles, wr_tiles):
            for q in range(nst):
                S.op("dve", lambda e, q=q: e.bn_stats(out=ST[:, q, :], in_=src_ap[:, q * fch:(q + 1) * fch]), reads=rd_tiles, writes=[ST])
            S.op("dve", lambda e: e.bn_aggr(out=MV[:, 0:2], in_=ST[:, :, :]), reads=[ST], writes=[MV])
            S.op("act", lambda e: e.activation(out=MV[:, 2:3], in_=MV[:, 1:2], func=AF.Ln, bias=1e-5, scale=1.0), reads=[MV], writes=[MV])
            S.op("act", lambda e: e.activation(out=MV[:, 2:3], in_=MV[:, 2:3], func=AF.Exp, scale=-0.5), reads=[MV], writes=[MV])
            S.op("dve", lambda e: e.tensor_scalar(out=dst_ap, in0=src_ap, scalar1=MV[:, 0:1], scalar2=MV[:, 2:3], op0=ALU.subtract, op1=ALU.mult),
                 reads=rd_tiles + [MV], writes=wr_tiles)
            S.op("dve", lambda e: e.tensor_tensor(out=dst_ap, in0=dst_ap, in1=G[:], op=ALU.mult), reads=wr_tiles + [G], writes=wr_tiles)
            S.op("dve", lambda e: e.tensor_tensor(out=dst_ap, in0=dst_ap, in1=Bb[:], op=ALU.add), reads=wr_tiles + [Bb], writes=wr_tiles)

        NTB = 384
        hscr = nc.dram_tensor("hscr", [NOWN, D], F32).ap()
        yscr = nc.dram_tensor("yscr", [NOWN, D], F32).ap()
        gate_scr = nc.dram_tensor("gate_scr", [8, 128, 4, NOWN], BF16).ap()
        HSCR = T(None)
        YSCR = T(None)
        GSCR = T(None)
        ht_scr = nc.dram_tensor("ht_scr", [128, 16, NOWN], BF16).ap()
        HTSCR = T(None)
        with ExitStack() as esg:
            XTa = XTO
            SGB = [S.sb([128, 4, NOWN], BF16, esg, name=f"sgb{i}") for i in range(2)]
            for wi in range(8):
                col = (C_GS if wi < 4 else C_GA) + (wi % 4) * 512
                wsl = load_w(w_in[:, col:col + 512], 16, 512)
                sgb = SGB[wi % 2]
                for ct in range(4):
                    for tg in range(3):
                        ps = PF()
                        for k in range(16):
                            S.op("pe", lambda e, k=k, ps=ps, ct=ct, tg=tg: e.matmul(ps[:, 0:NTB], lhsT=wsl[:, k, ct * 128:(ct + 1) * 128], rhs=XTa[:, k, tg * NTB:(tg + 1) * NTB],
                                                                                start=(k == 0), stop=(k == 15)), reads=[wsl, XTa], writes=[ps], sig=(k == 15))
                        S.op("act", lambda e, ps=ps, ct=ct, tg=tg: e.activation(out=sgb[:, ct, tg * NTB:(tg + 1) * NTB], in_=ps[:, 0:NTB], func=AF.Sigmoid), reads=[ps], writes=[sgb])
                S.dma("sp", gate_scr[wi], sgb[:, :, :], reads=[sgb], writes=[GSCR])
            S.barrier()
        esxo.close()
        esmix = ExitStack()
        MIX = S.sb([128, 16, NOWN], BF16, esmix, name="mixall")
        with ExitStack() as ess:
            SYa = S.sb([128, 32, NOWN], BF16, ess, name="sya")
            GB = [S.sb([128, 4, NOWN], BF16, ess, name=f"gb{i}") for i in range(2)]
            for g in range(8):
                S.dma("sp", SYa[:, g * 4:(g + 1) * 4, :], ssdy_scr[g], reads=[SSDYS], writes=[SYa])
            for cb in range(4):
                wso0 = load_w(w_so[0:2048, cb * 512:(cb + 1) * 512], 16, 512)
                wso1 = load_w(w_so[2048:4096, cb * 512:(cb + 1) * 512], 16, 512)
                gb = GB[cb % 2]
                S.dma("sp", gb[:, :, :], gate_scr[cb], reads=[GSCR], writes=[gb])
                for ct in range(4):
                    for tg in range(3):
                        ps = PF()
                        for k in range(32):
                            wsl = wso0 if k < 16 else wso1
                            S.op("pe", lambda e, k=k, ps=ps, wsl=wsl, ct=ct, tg=tg: e.matmul(ps[:, 0:NTB], lhsT=wsl[:, k % 16, ct * 128:(ct + 1) * 128], rhs=SYa[:, k, tg * NTB:(tg + 1) * NTB],
                                                                                         start=(k == 0), stop=(k == 31)), reads=[wsl, SYa], writes=[ps], sig=(k == 31))
                        S.op("dve", lambda e, ps=ps, ct=ct, tg=tg, cb=cb, gb=gb: e.tensor_tensor(out=MIX[:, cb * 4 + ct, tg * NTB:(tg + 1) * NTB], in0=ps[:, 0:NTB],
                                                                                                in1=gb[:, ct, tg * NTB:(tg + 1) * NTB], op=ALU.mult), reads=[ps, gb], writes=[MIX])
            S.barrier()
        with ExitStack() as esa5:
            ATa = S.sb([128, 16, NOWN], BF16, esa5, name="ata")
            GB = [S.sb([128, 4, NOWN], BF16, esa5, name=f"gba{i}") for i in range(2)]
            TCa = [S.sb([128, NTB], F32, esa5, name=f"tca{i}") for i in range(2)]
            S.dma("sp", ATa[:, :, :], att_scr[:, :, :], reads=[ATTS], writes=[ATa])
            ita = 0
            for cb in range(4):
                wao = load_w(w_ao[:, cb * 512:(cb + 1) * 512], 16, 512)
                gb = GB[cb % 2]
                S.dma("sp", gb[:, :, :], gate_scr[4 + cb], reads=[GSCR], writes=[gb])
                for ct in range(4):
                    for tg in range(3):
                        ita += 1
                        ps = PF()
                        for k in range(16):
                            S.op("pe", lambda e, k=k, ps=ps, ct=ct, tg=tg: e.matmul(ps[:, 0:NTB], lhsT=wao[:, k, ct * 128:(ct + 1) * 128], rhs=ATa[:, k, tg * NTB:(tg + 1) * NTB],
                                                                                start=(k == 0), stop=(k == 15)), reads=[wao, ATa], writes=[ps], sig=(k == 15))
                        tc_ = TCa[ita % 2]
                        S.op("dve", lambda e, ps=ps, ct=ct, tg=tg, gb=gb, tc_=tc_: e.tensor_tensor(out=tc_[:], in0=ps[:, 0:NTB], in1=gb[:, ct, tg * NTB:(tg + 1) * NTB], op=ALU.mult),
                             reads=[ps, gb], writes=[tc_])
                        S.op("dve", lambda e, ct=ct, tg=tg, cb=cb, tc_=tc_: e.tensor_tensor(out=MIX[:, cb * 4 + ct, tg * NTB:(tg + 1) * NTB], in0=tc_[:],
                                                                                           in1=MIX[:, cb * 4 + ct, tg * NTB:(tg + 1) * NTB], op=ALU.add), reads=[tc_, MIX], writes=[MIX])
            S.barrier()
        with ExitStack() as esb5:
            YS4 = [S.sb([128, 512], F32, esb5, name=f"ys4{i}") for i in range(2)]
            for cb in range(4):
                wo = load_w(w_o[:, cb * 512:(cb + 1) * 512], 16, 512)
                for tt in range(9):
                    ps = PF()
                    for k in range(16):
                        S.op("pe", lambda e, k=k, ps=ps, tt=tt: e.matmul(ps[:, 0:512], lhsT=MIX[:, k, tt * 128:(tt + 1) * 128], rhs=wo[:, k, 0:512],
                                                                     start=(k == 0), stop=(k == 15)), reads=[MIX, wo], writes=[ps], sig=(k == 15))
                    ys = YS4[tt % 2]
                    S.op("act", lambda e, ps=ps, ys=ys: e.copy(out=ys[:], in_=ps[:, 0:512]), reads=[ps], writes=[ys])
                    S.dma("sp", yscr[tt * 128:(tt + 1) * 128, cb * 512:(cb + 1) * 512], ys[:], reads=[ys], writes=[YSCR])
            LG = S.sb([128, D], F32, esb5, name="lng")
            LB = S.sb([128, D], F32, esb5, name="lnb")
            ST = S.sb([128, nst, SDIM], F32, esb5, name="bnst")
            MV = S.sb([128, 4], F32, esb5, name="bnmv")
            S.dma("sp", LG[:], ln1_g.partition_broadcast(128), writes=[LG])
            S.dma("sp", LB[:], ln1_b.partition_broadcast(128), writes=[LB])
            XR = [S.sb([128, D], F32, esb5, name=f"xr{i}") for i in range(2)]
            AR4 = [S.sb([128, D], F32, esb5, name=f"ar4{i}") for i in range(2)]
            HR4 = [S.sb([128, D], F32, esb5, name=f"hr4{i}") for i in range(2)]
            HB = [S.sb([128, D], BF16, esb5, name=f"hb{i}") for i in range(2)]
            HTS = [S.sb([128, 16, 128], BF16, esb5, name=f"hts{i}") for i in range(2)]
            def ld1(tt):
                S.dma("sp", XR[tt % 2][:], xo[tt * 128:(tt + 1) * 128, :], writes=[XR[tt % 2]])
                S.dma("sp", AR4[tt % 2][:], yscr[tt * 128:(tt + 1) * 128, :], reads=[YSCR], writes=[AR4[tt % 2]])

            ld1(0)
            for tt in range(9):
                xr, ar, hr, hb = XR[tt % 2], AR4[tt % 2], HR4[tt % 2], HB[tt % 2]
                if tt + 1 < 9:
                    ld1(tt + 1)
                S.op("dve", lambda e, xr=xr, ar=ar: e.scalar_tensor_tensor(out=ar[:], in0=xr[:], scalar=float(ALPHA), in1=ar[:], op0=ALU.mult, op1=ALU.add),
                     reads=[xr, ar], writes=[ar])
                layer_norm_rows(ar[:], hr[:], LG, LB, ST, MV, [ar], [hr])
                S.op("act", lambda e, hb=hb, hr=hr: e.copy(out=hb[:], in_=hr[:]), reads=[hr], writes=[hb])
                S.dma("sp", hscr[tt * 128:(tt + 1) * 128, :], hr[:], reads=[hr], writes=[HSCR])
                for half in range(2):
                    pb = PB()
                    for j in range(8):
                        k = half * 8 + j
                        S.op("pe", lambda e, k=k, j=j, pb=pb, hb=hb: e.transpose(pb[:, j * 128:(j + 1) * 128], hb[:, k * 128:(k + 1) * 128], identb),
                             reads=[hb], writes=[pb], sig=(j == 7))
                    hts = HTS[tt % 2]
                    if half == 0:
                        S.op("act", lambda e, pb=pb, hts=hts: e.copy(out=hts[:, 0:8, :], in_=pb[:, :].rearrange("p (k t) -> p k t", k=8)), reads=[pb], writes=[hts])
                    else:
                        S.op("dve", lambda e, pb=pb, hts=hts: e.tensor_copy(out=hts[:, 8:16, :], in_=pb[:, :].rearrange("p (k t) -> p k t", k=8)), reads=[pb], writes=[hts])
                S.dma("sp", ht_scr[:, :, tt * 128:(tt + 1) * 128], HTS[tt % 2][:, :, :], reads=[HTS[tt % 2]], writes=[HTSCR])
            S.barrier()
        esmix.close()
        HT_ = S.sb([128, 16, NOWN], BF16, name="htall")
        S.dma("sp", HT_[:, :, :], ht_scr[:, :, :], reads=[HTSCR], writes=[HT_])
        with ExitStack() as es5:
            ACT_ = S.sb([128, 44, NOWN], BF16, es5, name="actt")
            SGT = [S.sb([128, NTB], F32, es5, name=f"sgt{i}") for i in range(2)]
            YS = [S.sb([128, 512], F32, es5, name=f"ys{i}") for i in range(2)]
            it5 = 0
            for fb in range(11):
                wg = load_w(w_gu[:, fb * 512:(fb + 1) * 512], 16, 512)
                wu = load_w(w_gu[:, DFF + fb * 512:DFF + (fb + 1) * 512], 16, 512)
                for ct in range(4):
                    for tg in range(3):
                        it5 += 1
                        psg = PF()
                        for k in range(16):
                            S.op("pe", lambda e, k=k, psg=psg, ct=ct, tg=tg: e.matmul(psg[:, 0:NTB], lhsT=wg[:, k, ct * 128:(ct + 1) * 128], rhs=HT_[:, k, tg * NTB:(tg + 1) * NTB],
                                                                                  start=(k == 0), stop=(k == 15)), reads=[wg, HT_], writes=[psg], sig=(k == 15))
                        psu = PF()
                        for k in range(16):
                            S.op("pe", lambda e, k=k, psu=psu, ct=ct, tg=tg: e.matmul(psu[:, 0:NTB], lhsT=wu[:, k, ct * 128:(ct + 1) * 128], rhs=HT_[:, k, tg * NTB:(tg + 1) * NTB],
                                                                                  start=(k == 0), stop=(k == 15)), reads=[wu, HT_], writes=[psu], sig=(k == 15))
                        sgt = SGT[it5 % 2]
                        S.op("act", lambda e, psg=psg, sgt=sgt: e.activation(out=sgt[:], in_=psg[:, 0:NTB], func=AF.Silu), reads=[psg], writes=[sgt])
                        S.op("dve", lambda e, psu=psu, sgt=sgt, fb=fb, ct=ct, tg=tg: e.tensor_tensor(out=ACT_[:, fb * 4 + ct, tg * NTB:(tg + 1) * NTB], in0=psu[:, 0:NTB], in1=sgt[:], op=ALU.mult),
                             reads=[psu, sgt], writes=[ACT_])
            for cb in range(4):
                wd = [load_w(w_dn[0:2048, cb * 512:(cb + 1) * 512], 16, 512), load_w(w_dn[2048:4096, cb * 512:(cb + 1) * 512], 16, 512),
                      load_w(w_dn[4096:DFF, cb * 512:(cb + 1) * 512], 12, 512)]
                for tt in range(9):
                    ps = PF()
                    for k in range(44):
                        wsl = wd[k // 16]
                        S.op("pe", lambda e, k=k, ps=ps, tt=tt, wsl=wsl: e.matmul(ps[:, 0:512], lhsT=ACT_[:, k, tt * 128:(tt + 1) * 128], rhs=wsl[:, k % 16, 0:512],
                                                                              start=(k == 0), stop=(k == 43)), reads=[ACT_, wsl], writes=[ps], sig=(k == 43))
                    ys = YS[tt % 2]
                    S.op("act", lambda e, ps=ps, ys=ys: e.copy(out=ys[:], in_=ps[:, 0:512]), reads=[ps], writes=[ys])
                    S.dma("sp", yscr[tt * 128:(tt + 1) * 128, cb * 512:(cb + 1) * 512], ys[:], reads=[ys], writes=[YSCR])
            S.barrier()
        with ExitStack() as es6:
            LG = S.sb([128, D], F32, es6, name="lng2")
            LB = S.sb([128, D], F32, es6, name="lnb2")
            ST = S.sb([128, nst, SDIM], F32, es6, name="bnst2")
            MV = S.sb([128, 4], F32, es6, name="bnmv2")
            S.dma("sp", LG[:], ln2_g.partition_broadcast(128), writes=[LG])
            S.dma("sp", LB[:], ln2_b.partition_broadcast(128), writes=[LB])
            AR = [S.sb([128, D], F32, es6, name=f"ar{i}") for i in range(2)]
            HR = [S.sb([128, D], F32, es6, name=f"hr{i}") for i in range(2)]
            YT = [S.sb([128, D], F32, es6, name=f"yt{i}") for i in range(2)]
            def ld2(tt):
                S.dma("sp", AR[tt % 2][:], yscr[tt * 128:(tt + 1) * 128, :], reads=[YSCR], writes=[AR[tt % 2]])
                S.dma("sp", HR[tt % 2][:], hscr[tt * 128:(tt + 1) * 128, :], reads=[HSCR], writes=[HR[tt % 2]])

            ld2(0)
            for tt in range(9):
                ar, hr, yt = AR[tt % 2], HR[tt % 2], YT[tt % 2]
                if tt + 1 < 9:
                    ld2(tt + 1)
                S.op("dve", lambda e, ar=ar, hr=hr: e.scalar_tensor_tensor(out=ar[:], in0=hr[:], scalar=float(ALPHA), in1=ar[:], op0=ALU.mult, op1=ALU.add),
                     reads=[hr, ar], writes=[ar])
                layer_norm_rows(ar[:], yt[:], LG, LB, ST, MV, [ar], [yt])
                S.dma("sp", y_o[tt * 128:(tt + 1) * 128, :], yt[:], reads=[yt])
            S.barrier()
        S.barrier()
    return nc


_CACHE = {}


def _host_consts():
    cst = np.zeros((128, 512), np.float32)
    cst[:, 0:128] = np.eye(128, dtype=np.float32)
    k = np.arange(64)
    cst[0:64, 128:192] = (k[:, None] <= k[None, :]).astype(np.float32)
    cst[0:64, 192:256] = (k[:, None] > k[None, :]).astype(np.float32)
    cst[:, 256:384] = 1.0
    k2 = np.arange(128)
    cst[:, 384:512] = (k2[:, None] > k2[None, :]).astype(np.float32)
    return cst


def _bias_tables(rel_bias):
    rb = np.asarray(rel_bias, np.float32)
    p = np.arange(128)[:, None, None]
    tt = np.arange(2)[None, :, None]
    i = np.arange(64)[None, None, :]
    out = np.zeros((3, 128, 16, 2, 64), np.float32)
    for v in range(3):
        if v == 0:
            jb = 128 * (3 + tt) + p + 0 * i
        elif v == 1:
            jb = 128 * (3 + tt) + p - 64 + 0 * i
        else:
            jb = np.where(tt == 0, 384 + p, 512 + (p - 64)) + 0 * i
            jb = np.where((tt == 1) & (p < 64), 10000, jb)
        idx = np.minimum(i - jb + 640, 256)
        idx = np.where((jb < 0) | (jb >= 576), 256, idx)
        idx = np.clip(idx, 0, 256)
        out[v] = np.transpose(rb[:, idx], (1, 0, 2, 3))
    return out.reshape(3, 128, 16 * 128)


def kernel(x_prompt, x_sample, cache_k, cache_v, state_conv, state_ssm, w_in, conv_w, conv_b,
           dt_bias, a_log, d_skip, ssd_norm_w, rel_bias, w_ssd_out, w_att_out, w_o, ln1_g, ln1_b,
           w_gate_up, w_down, ln2_g, ln2_b):
    f = lambda a: np.ascontiguousarray(np.asarray(a, dtype=np.float32))
    x_prompt = f(x_prompt); x_sample = f(x_sample)
    cache_k = f(cache_k); cache_v = f(cache_v); state_conv = f(state_conv); state_ssm = f(state_ssm)
    if "nc" not in _CACHE:
        _CACHE["nc"] = build_program()
    nc = _CACHE["nc"]
    shared = dict(
        xp=f(x_prompt[0, :NPRE * NPR]), w_in=f(w_in[0]), conv_w=f(conv_w[0]), conv_b=f(conv_b[0]),
        dt_bias=f(dt_bias[0]), a_log=f(a_log[0]), d_skip=f(d_skip[0]), norm_w=f(ssd_norm_w[0]),
        bt=_bias_tables(np.asarray(rel_bias)[0]), relc=f(np.asarray(rel_bias)[0][:, 256]),
        w_so=f(w_ssd_out[0]), w_ao=f(w_att_out[0]), w_o=f(w_o[0]), ln1_g=f(ln1_g[0]), ln1_b=f(ln1_b[0]),
        ln2_g=f(ln2_g[0]), ln2_b=f(ln2_b[0]), w_gu=f(w_gate_up[0]), w_dn=f(w_down[0]), cst=_host_consts())
    in_maps = []
    for c in range(NC_):
        m = dict(shared)
        m["xo"] = np.concatenate([x_prompt[0, NPR * c:NPR * (c + 1)], x_sample[2 * c], x_sample[2 * c + 1]], axis=0)
        m["xh"] = x_prompt[0, NPR * c - NHALO:NPR * c] if c > 0 else np.zeros((NHALO, D), np.float32)
        m["ck"] = cache_k[0, 2 * c:2 * c + 2].reshape(2, 512, D)
        m["cv"] = cache_v[0, 2 * c:2 * c + 2].reshape(2, 512, D)
        m["sconv"] = state_conv[0, 2 * c:2 * c + 2]
        m["sssm"] = state_ssm[0, 2 * c:2 * c + 2].reshape(2, 4096, 128)
        m["hones"] = np.full((128, 128), 1.0 if c > 0 else 0.0, np.float32)
        pm = np.zeros((128, NPRE), np.float32)
        pm[:, :min(c, NPRE)] = 1.0
        m["pmask"] = pm
        in_maps.append({k: np.ascontiguousarray(v) for k, v in m.items()})
    if DEBUG_CORES is None:
        res = run_bass_kernel_spmd(nc, in_maps, core_ids=list(range(NC_)))
        R = res.results
    else:
        res = run_bass_kernel_spmd(nc, [in_maps[c] for c in DEBUG_CORES], core_ids=list(range(len(DEBUG_CORES))), trace=DEBUG_TRACE)
        LAST_EXEC_NS[0] = res.exec_time_ns
        R = [res.results[DEBUG_CORES.index(c)] if c in DEBUG_CORES else res.results[0] for c in range(NC_)]
    y_p = np.concatenate([R[c]["y"][:NPR] for c in range(NC_)], axis=0)[None]
    y_s = np.stack([R[c]["y"][NPR + 64 * s:NPR + 64 * (s + 1)] for c in range(NC_) for s in range(2)], axis=0)
    conv_p = R[7]["convp"][None, None]
    ssm_p = R[7]["ssmp"].reshape(1, 1, 64, 64, 128)
    k_p = R[7]["kp"].reshape(1, 1, 512, 16, 128)
    v_p = R[7]["vp"].reshape(1, 1, 512, 16, 128)
    conv_s = np.stack([R[c]["convs"][s] for c in range(NC_) for s in range(2)], axis=0)[None]
    ssm_s = np.stack([R[c]["ssms"][s].reshape(64, 64, 128) for c in range(NC_) for s in range(2)], axis=0)[None]
    k_s = np.stack([R[c]["ks"][64 * s:64 * (s + 1)].reshape(64, 16, 128) for c in range(NC_) for s in range(2)], axis=0)[None]
    v_s = np.stack([R[c]["vs"][64 * s:64 * (s + 1)].reshape(64, 16, 128) for c in range(NC_) for s in range(2)], axis=0)[None]
    return tuple(np.ascontiguousarray(a.astype(np.float32)) for a in (y_p, y_s, conv_p, ssm_p, k_p, v_p, conv_s, ssm_s, k_s, v_s))
```
